# Optimizing a Trainium2 kernel written in Bass

```python
import math
import jax, jax.numpy as jnp
from jax import lax
import numpy as np

D_MODEL = 1024
BATCH = 2
SEQ = 8192
DEPTH = 4
DEC_BATCH = 128
DEC_SEQ = 1
PAST_LEN = 8192
PAGE_SIZE = 128

N_A = DEPTH // 2
N_B = DEPTH - N_A
CONV_W = 3
N_HEADS = 16
N_KV_HEADS = 4
HEAD_DIM = D_MODEL // N_HEADS
GROUP = N_HEADS // N_KV_HEADS
ROT_DIM = HEAD_DIM // 4
ROPE_THETA = 500000.0
WINDOW = 128
BLOCK = 128
D_FF = 2816
EPS = 1e-6
W_BUF = min(WINDOW, PAST_LEN)

kernel_name = "yoco_shortconv_swa_sink_macaron"


def _rmsnorm(x, g):
    xf = x.astype(jnp.float32)
    y = xf * lax.rsqrt(jnp.mean(xf * xf, axis=-1, keepdims=True) + EPS)
    return (y * g.astype(jnp.float32)).astype(x.dtype)


def _swiglu(h, w_gate, w_up, w_down):
    return (jax.nn.silu(h @ w_gate) * (h @ w_up)) @ w_down


def _rope(x, pos):
    half = ROT_DIM // 2
    inv_freq = ROPE_THETA ** (-jnp.arange(0, ROT_DIM, 2, dtype=jnp.float32) / ROT_DIM)
    ang = pos.astype(jnp.float32)[:, None] * inv_freq[None, :]
    cos = jnp.cos(ang)[:, None, :]
    sin = jnp.sin(ang)[:, None, :]
    xr = x[..., :ROT_DIM].astype(jnp.float32)
    x1, x2 = xr[..., :half], xr[..., half:]
    rot = jnp.concatenate([x1 * cos - x2 * sin, x2 * cos + x1 * sin], axis=-1)
    return jnp.concatenate([rot.astype(x.dtype), x[..., ROT_DIM:]], axis=-1)


def _short_conv_mixer(h, conv_state, w_in, w_conv, w_out):
    S = h.shape[1]
    b, c, u = jnp.split(h @ w_in, 3, axis=-1)
    cu = c * u
    ext = jnp.concatenate([conv_state.astype(cu.dtype), cu], axis=1)
    conv = sum(w_conv[j] * ext[:, j:j + S] for j in range(CONV_W))
    y = (b * conv) @ w_out
    return y, ext[:, -(CONV_W - 1):]


def _shared_kv(x, pos, g_kv, w_kv, g_knorm):
    B, S, _ = x.shape
    kv = _rmsnorm(x, g_kv) @ w_kv
    k, v = jnp.split(kv, 2, axis=-1)
    k = k.reshape(B, S, N_KV_HEADS, HEAD_DIM)
    v = v.reshape(B, S, N_KV_HEADS, HEAD_DIM)
    k = _rope(_rmsnorm(k, g_knorm), pos)
    return k, v


def _queries(h, pos, w_q, g_qnorm):
    B, S, _ = h.shape
    q = (h @ w_q).reshape(B, S, N_HEADS, HEAD_DIM)
    return _rope(_rmsnorm(q, g_qnorm), pos)


def _sink_weights(s, valid, sinks):
    sk = sinks.astype(jnp.float32).reshape(N_KV_HEADS, GROUP)[:, :, None, None]
    s = jnp.where(valid, s, -jnp.inf)
    m = jnp.maximum(jnp.max(s, axis=-1, keepdims=True), sk)
    p = jnp.exp(s - m)
    return p / (jnp.sum(p, axis=-1, keepdims=True) + jnp.exp(sk - m))


def _banded_window_attention(q, k, v, sinks):
    B, S, _, _ = q.shape
    nblk = S // BLOCK
    qb = q.reshape(B, nblk, BLOCK, N_KV_HEADS, GROUP, HEAD_DIM)
    kb = k.reshape(B, nblk, BLOCK, N_KV_HEADS, HEAD_DIM)
    vb = v.reshape(B, nblk, BLOCK, N_KV_HEADS, HEAD_DIM)
    kk = jnp.concatenate([jnp.concatenate([jnp.zeros_like(kb[:, :1]), kb[:, :-1]], axis=1), kb], axis=2)
    vv = jnp.concatenate([jnp.concatenate([jnp.zeros_like(vb[:, :1]), vb[:, :-1]], axis=1), vb], axis=2)
    s = jnp.einsum('bnqhgd,bnshd->bnhgqs', qb, kk).astype(jnp.float32) * (1.0 / math.sqrt(HEAD_DIM))
    qi = jnp.arange(BLOCK)[:, None] + BLOCK
    kj = jnp.arange(2 * BLOCK)[None, :]
    rel = qi - kj
    band = (rel >= 0) & (rel < WINDOW)
    kpos = (jnp.arange(nblk)[:, None, None] - 1) * BLOCK + kj[None]
    valid = (band[None] & (kpos >= 0))[None, :, None, None]
    w = _sink_weights(s, valid, sinks)
    o = jnp.einsum('bnhgqs,bnshd->bnqhgd', w.astype(v.dtype), vv)
    return o.reshape(B, S, N_HEADS * HEAD_DIM)


def _cached_window_attention(q, k_new, v_new, cache_k, cache_v, sinks):
    Bd, Sd, _, _ = q.shape
    kk = jnp.concatenate([cache_k.astype(k_new.dtype), k_new], axis=1)
    vv = jnp.concatenate([cache_v.astype(v_new.dtype), v_new], axis=1)
    qg = q.reshape(Bd, Sd, N_KV_HEADS, GROUP, HEAD_DIM)
    s = jnp.einsum('bqhgd,bshd->bhgqs', qg, kk).astype(jnp.float32) * (1.0 / math.sqrt(HEAD_DIM))
    qpos = PAST_LEN + jnp.arange(Sd)
    kpos = jnp.concatenate([PAST_LEN - W_BUF + jnp.arange(W_BUF), PAST_LEN + jnp.arange(Sd)])
    rel = qpos[:, None] - kpos[None, :]
    valid = ((rel >= 0) & (rel < WINDOW))[None, None, None]
    w = _sink_weights(s, valid, sinks)
    o = jnp.einsum('bhgqs,bshd->bqhgd', w.astype(vv.dtype), vv)
    return o.reshape(Bd, Sd, N_HEADS * HEAD_DIM)


def _trunk(x, pos, conv_in, cache_k, cache_v,
           g_ffn1, w_ffn1_gate, w_ffn1_up, w_ffn1_down, g_mix,
           g_ffn2, w_ffn2_gate, w_ffn2_up, w_ffn2_down,
           w_in_a, conv_w, w_out_a, g_kv, w_kv, g_knorm,
           w_q, g_qnorm, sinks, w_o):
    new_conv = []
    k = v = None
    for i in range(DEPTH):
        if i == N_A:
            k, v = _shared_kv(x, pos, g_kv, w_kv, g_knorm)
        x = x + 0.5 * _swiglu(_rmsnorm(x, g_ffn1[i]), w_ffn1_gate[i], w_ffn1_up[i], w_ffn1_down[i])
        h = _rmsnorm(x, g_mix[i])
        if i < N_A:
            y, st = _short_conv_mixer(h, conv_in[i], w_in_a[i], conv_w[i], w_out_a[i])
            new_conv.append(st)
        else:
            j = i - N_A
            q = _queries(h, pos, w_q[j], g_qnorm[j])
            if cache_k is None:
                o = _banded_window_attention(q, k, v, sinks[j])
            else:
                o = _cached_window_attention(q, k, v, cache_k, cache_v, sinks[j])
            y = o @ w_o[j]
        x = x + y
        x = x + 0.5 * _swiglu(_rmsnorm(x, g_ffn2[i]), w_ffn2_gate[i], w_ffn2_up[i], w_ffn2_down[i])
    if cache_k is None:
        kb, vb = k[:, -W_BUF:], v[:, -W_BUF:]
    else:
        kb = jnp.concatenate([cache_k.astype(k.dtype), k], axis=1)[:, -W_BUF:]
        vb = jnp.concatenate([cache_v.astype(v.dtype), v], axis=1)[:, -W_BUF:]
    return x, jnp.stack(new_conv), kb, vb


def setup_inputs(seed: int = 0) -> dict:
    key = jax.random.key(seed)
    ks = jax.random.split(key, 24)

    def nrm(k, shape, scale):
        return jax.random.normal(k, shape, jnp.float32) * scale

    HKV = N_KV_HEADS * HEAD_DIM
    HQ = N_HEADS * HEAD_DIM
    return {
        "x_prompt": nrm(ks[0], (BATCH, SEQ, D_MODEL), 1.0),
        "x_sample": nrm(ks[1], (DEC_BATCH, DEC_SEQ, D_MODEL), 1.0),
        "state_conv": nrm(ks[2], (N_A, DEC_BATCH, CONV_W - 1, D_MODEL), 1.0),
        "cache_k": nrm(ks[3], (DEC_BATCH, W_BUF, N_KV_HEADS, HEAD_DIM), 1.0),
        "cache_v": nrm(ks[4], (DEC_BATCH, W_BUF, N_KV_HEADS, HEAD_DIM), 1.0),
        "g_ffn1": 1.0 + nrm(ks[5], (DEPTH, D_MODEL), 0.02),
        "w_ffn1_gate": nrm(ks[6], (DEPTH, D_MODEL, D_FF), D_MODEL ** -0.5),
        "w_ffn1_up": nrm(ks[7], (DEPTH, D_MODEL, D_FF), D_MODEL ** -0.5),
        "w_ffn1_down": nrm(ks[8], (DEPTH, D_FF, D_MODEL), D_FF ** -0.5),
        "g_mix": 1.0 + nrm(ks[9], (DEPTH, D_MODEL), 0.02),
        "g_ffn2": 1.0 + nrm(ks[10], (DEPTH, D_MODEL), 0.02),
        "w_ffn2_gate": nrm(ks[11], (DEPTH, D_MODEL, D_FF), D_MODEL ** -0.5),
        "w_ffn2_up": nrm(ks[12], (DEPTH, D_MODEL, D_FF), D_MODEL ** -0.5),
        "w_ffn2_down": nrm(ks[13], (DEPTH, D_FF, D_MODEL), D_FF ** -0.5),
        "w_in_a": nrm(ks[14], (N_A, D_MODEL, 3 * D_MODEL), D_MODEL ** -0.5),
        "conv_w": nrm(ks[15], (N_A, CONV_W, D_MODEL), CONV_W ** -0.5),
        "w_out_a": nrm(ks[16], (N_A, D_MODEL, D_MODEL), D_MODEL ** -0.5),
        "g_kv": 1.0 + nrm(ks[17], (D_MODEL,), 0.02),
        "w_kv": nrm(ks[18], (D_MODEL, 2 * HKV), D_MODEL ** -0.5),
        "g_knorm": 1.0 + nrm(ks[19], (HEAD_DIM,), 0.02),
        "w_q": nrm(ks[20], (N_B, D_MODEL, HQ), D_MODEL ** -0.5),
        "g_qnorm": 1.0 + nrm(ks[21], (N_B, HEAD_DIM), 0.02),
        "sinks": nrm(ks[22], (N_B, N_HEADS), 1.0),
        "w_o": nrm(ks[23], (N_B, HQ, D_MODEL), HQ ** -0.5),
    }


def reference(x_prompt, x_sample, state_conv, cache_k, cache_v,
              g_ffn1, w_ffn1_gate, w_ffn1_up, w_ffn1_down, g_mix,
              g_ffn2, w_ffn2_gate, w_ffn2_up, w_ffn2_down,
              w_in_a, conv_w, w_out_a, g_kv, w_kv, g_knorm,
              w_q, g_qnorm, sinks, w_o):
    weights = (g_ffn1, w_ffn1_gate, w_ffn1_up, w_ffn1_down, g_mix,
               g_ffn2, w_ffn2_gate, w_ffn2_up, w_ffn2_down,
               w_in_a, conv_w, w_out_a, g_kv, w_kv, g_knorm,
               w_q, g_qnorm, sinks, w_o)
    pos_p = jnp.arange(SEQ, dtype=jnp.int32)
    conv0 = jnp.zeros((N_A, x_prompt.shape[0], CONV_W - 1, D_MODEL), x_prompt.dtype)
    y_prompt, conv_p, k_p, v_p = _trunk(x_prompt, pos_p, conv0, None, None, *weights)
    pos_s = PAST_LEN + jnp.arange(x_sample.shape[1], dtype=jnp.int32)
    y_sample, conv_s, k_s, v_s = _trunk(x_sample, pos_s, state_conv, cache_k, cache_v, *weights)
    return (y_prompt, y_sample, conv_p, k_p, v_p, conv_s, k_s, v_s)
```

```python
import contextlib
import math

import numpy as np
import concourse.bass as bass
import concourse.mybir as mybir
from concourse.bass_utils import run_bass_kernel_spmd

F32 = mybir.dt.float32
BF16 = mybir.dt.bfloat16
AF = mybir.ActivationFunctionType
ALU = mybir.AluOpType

D = 1024
DFF = 2816
NCH = 8
HALO = 132
OWN = 1024
NS = 8
T = HALO + OWN + NS
NPASS = 2
NCORES = 8
EPS = 1e-6
A_TILES = [(0, 388), (388, 388), (776, 388)]
B_TILES = [(132, 344), (476, 344), (820, 344)]
TW = 388
FFN_GROUPS = [(0, 4), (4, 3), (7, 3), (10, 4), (14, 4), (18, 4)]
MAGIC = 12582912.0
TWO_PI_SAFE = 6.28318
NSM = 176
SAME_ENG_SYNC = True
import os
DBG = set(os.environ.get('KDBG', '').split(','))


class Res:
    __slots__ = ("name", "w", "r", "dsem", "dcount")

    def __init__(self, name):
        self.name = name
        self.w = None
        self.r = {}
        self.dsem = None
        self.dcount = 0


class Eng:
    def __init__(self, P, name, h, is_pe=False):
        self.P = P
        self.name = name
        self.h = h
        self.sem = P.es.enter_context(P.nc.semaphore("e_" + name))
        self.count = 0
        self.seen = {}
        self.is_pe = is_pe
        self.key = ("eng", name)


class Prog:
    def __init__(self):
        self.nc = bass.Bass("TRN2", target_bir_lowering=False)
        self.es = contextlib.ExitStack()
        self.res = {}
        self.pending = None
        self.on_flush = []
        nc = self.nc
        self.PE = Eng(self, "pe", nc.tensor, is_pe=True)
        self.ACT = Eng(self, "act", nc.scalar)
        self.DVE = Eng(self, "dve", nc.vector)
        self.POOL = Eng(self, "pool", nc.gpsimd)
        self.SP = Eng(self, "sp", nc.sync)
        self.engs = [self.PE, self.ACT, self.DVE, self.POOL, self.SP]
        self.out_res = []
        self.ucount = 0
        self.bcount = 0
        self.marks = []
        self.after_gu = []
        self.trickle = []
        self.npe = 0
        self.hook = None
        self.skip_norm = False

    def sb(self, name, shape, dt):
        return self.es.enter_context(self.nc.sbuf_tensor(name, list(shape), dt))

    def psum(self, name, shape, dt):
        return self.es.enter_context(self.nc.psum_tensor(name, list(shape), dt))

    def din(self, name, shape):
        return self.nc.dram_tensor(name, list(shape), F32, kind="ExternalInput").ap()

    def dout(self, name, shape):
        return self.nc.dram_tensor(name, list(shape), F32, kind="ExternalOutput").ap()

    def R(self, name):
        r = self.res.get(name)
        if r is None:
            r = Res(name)
            self.res[name] = r
        return r

    def _rl(self, xs):
        return [self.R(x) if isinstance(x, str) else x for x in xs]

    def _wait(self, eng, dep):
        sem, val, key = dep
        if key == eng.key and (eng.is_pe or not SAME_ENG_SYNC):
            return
        if eng.seen.get(key, 0) >= val:
            return
        eng.h.wait_ge(sem, val)
        eng.seen[key] = val

    def _deps(self, eng, reads, writes):
        for r in reads:
            if r.w is not None:
                self._wait(eng, r.w)
        for w in writes:
            if w.w is not None:
                self._wait(eng, w.w)
            for dep in list(w.r.values()):
                self._wait(eng, dep)

    def _record(self, stamp, reads, writes):
        key = stamp[2]
        for r in reads:
            r.r[key] = stamp
        for w in writes:
            w.w = stamp
            w.r = {}

    def op(self, eng, fn, reads=(), writes=()):
        reads = self._rl(reads)
        writes = self._rl(writes)
        bank_reads = [r for r in reads if r.name.startswith("bank")]
        if bank_reads:
            reads = [r for r in reads if not r.name.startswith("bank")]
            writes = writes + [r for r in bank_reads if r not in writes]
        self._deps(eng, reads, writes)
        if eng.is_pe:
            n0 = self._pecount()
        inst = fn()
        eng.count += 1
        inst.then_inc(eng.sem, 1)
        self._record((eng.sem, eng.count, eng.key), reads, writes)
        return inst

    def _pecount(self):
        return 0

    def mark(self, label):
        self.marks.append((self.PE.count, label))

    def dma(self, q, out, in_, reads=(), writes=(), track=None):
        return self.dma_batch(q, [(out, in_)], reads, writes, track)

    def dma_batch(self, q, pairs, reads=(), writes=(), track=None):
        reads = self._rl(reads)
        writes = self._rl(writes)
        track = self.R(track) if isinstance(track, str) else track
        self._deps(q, reads, writes)
        if track.dsem is None:
            track.dsem = self.es.enter_context(self.nc.semaphore("d_" + track.name))
        prev = (track.dsem, track.dcount, ("dma", track.name))
        if track.dcount > 0:
            self._wait(q, prev)
        for (out, in_) in pairs:
            inst = q.h.dma_start(out=out, in_=in_)
            track.dcount += 16
            inst.then_inc(track.dsem, 16)
        self._record((track.dsem, track.dcount, ("dma", track.name)), reads, writes)
        return inst

    def fence(self):
        for e in self.engs:
            for o in self.engs:
                if o is not e and o.count > 0:
                    self._wait(e, (o.sem, o.count, o.key))

    def run_after_gu(self):
        fl = self.after_gu
        self.after_gu = []
        for f in fl:
            f()

    def run_trickle(self, n=None):
        k = len(self.trickle) if n is None else min(n, len(self.trickle))
        fl = self.trickle[:k]
        self.trickle = self.trickle[k:]
        for f in fl:
            f()

    def flush(self):
        self.run_after_gu()
        self.run_trickle()
        if self.pending is not None:
            f = self.pending
            self.pending = None
            f()
        fl = self.on_flush
        self.on_flush = []
        for f in fl:
            f()

    def drain(self):
        self.flush()
        self.run_after_gu()
        self.run_trickle()

    def unit(self, gu_fn, d_fn):
        gu_fn()
        self.run_after_gu()
        self.run_trickle()
        self.flush()
        self.pending = d_fn


def build_program(stop_after=None):
    P = Prog()
    nc = P.nc
    PE, ACT, DVE, POOL, SP = P.PE, P.ACT, P.DVE, P.POOL, P.SP
    op, dma = P.op, P.dma

    xin = P.din("xin", [NPASS, T, D])
    posd = P.din("pos", [NPASS, 128, T])
    mask4d = P.din("mask4", [128, 512])
    mask40d = P.din("mask40", [NPASS, 128, 512])
    identd = P.din("ident", [128, 128])
    smallsd = P.din("smalls", [128, NSM])
    stcd = P.din("stc", [NPASS, 2, 2 * NS, D])
    ckd = P.din("ck", [NPASS, NS, 128, 256])
    cvd = P.din("cv", [NPASS, NS, 128, 256])
    w1g = P.din("w1g", [4, D, DFF]); w1u = P.din("w1u", [4, D, DFF]); w1d = P.din("w1d", [4, DFF, D])
    w2g = P.din("w2g", [4, D, DFF]); w2u = P.din("w2u", [4, D, DFF]); w2d = P.din("w2d", [4, DFF, D])
    wind = P.din("win", [2, D, 3 * D]); woutd = P.din("wout", [2, D, D])
    wkvd = P.din("wkv", [D, 512]); wksd = P.din("wks", [D, 256])
    wqpd = P.din("wqp", [2, D, D]); wqsd = P.din("wqs", [2, D, D]); wopd = P.din("wop", [2, D, D])

    y_own = P.dout("y_own", [NPASS, OWN, D])
    y_s = P.dout("y_s", [NPASS, NS, D])
    conv_s_o = P.dout("conv_s_o", [NPASS, 2, NS, 2, D])
    conv_p_o = P.dout("conv_p_o", [NPASS, 2, 2, D])
    k_p_o = P.dout("k_p_o", [NPASS, 128, 256])
    v_p_o = P.dout("v_p_o", [NPASS, 128, 256])
    k_s_o = P.dout("k_s_o", [NPASS, NS, 128, 256])
    v_s_o = P.dout("v_s_o", [NPASS, NS, 128, 256])

    x_sb = P.sb("x_sb", [128, NCH, T], F32)
    h_sb = P.sb("h_sb", [128, NCH, T], BF16)
    wA = [P.sb("wA%d" % i, [128, 8, 1024], BF16) for i in range(2)]
    wB = [P.sb("wB%d" % i, [128, 4, 1024], BF16) for i in range(2)]
    act = [P.sb("act%d" % i, [128, 4, TW], BF16) for i in range(2)]
    sg = [P.sb("sg%d" % i, [128, TW], BF16) for i in range(2)]
    sq = P.sb("sq", [128, NCH, TW], BF16)
    lnt = P.sb("lnt", [128, TW], F32)
    qsq = P.sb("qsq", [128, 2, TW], BF16)
    qrs = P.sb("qrs", [128, TW], F32)
    rstd = [P.sb("rstd%d" % i, [128, TW], F32) for i in range(2)]
    cosT = P.sb("cosT", [128, T], F32)
    sinT = P.sb("sinT", [128, T], F32)
    pos_sb = P.sb("pos_sb", [128, T], F32)
    kT = P.sb("kT", [128, 2, T], BF16)
    V_sb = P.sb("V_sb", [128, 9, 256], BF16)
    Kc = P.sb("Kc", [128, NS, 256], BF16)
    Vc = P.sb("Vc", [128, NS, 256], BF16)
    KTa = P.sb("KTa", [128, 2, NS, 128], BF16)
    qn = [P.sb("qn%d" % i, [128, OWN + NS], BF16) for i in range(2)]
    oT = [P.sb("oT%d" % i, [128, OWN + NS], BF16) for i in range(4)]
    t1 = P.sb("t1", [128, TW], F32)
    t2 = P.sb("t2", [128, TW], F32)
    PT = [P.sb("PT%d" % i, [128, 512], BF16) for i in range(4)]
    mask4 = P.sb("mask4s", [128, 512], BF16)
    mask40 = P.sb("mask40s", [128, 512], BF16)
    xs = [P.sb("xs%d" % i, [128, OWN + NS], F32) for i in range(2)]
    c_sb = P.sb("c_sb", [128, TW], F32)
    cu = [P.sb("cu%d" % i, [128, TW + 2], F32) for i in range(2)]
    conv = P.sb("conv", [128, TW], F32)
    zb = [P.sb("zb%d" % i, [128, 2, TW], BF16) for i in range(2)]
    st_fm = P.sb("st_fm", [128, 2, NCH, 2 * NS], F32)
    sv = P.sb("sv", [128, 2, NCH, NS + 2], F32)
    ktail = P.sb("ktail", [128, 2, 136], F32)
    kp_tok = P.sb("kp_tok", [128, 256], F32)
    vp_tok = P.sb("vp_tok", [128, 256], F32)
    ks_tok = P.sb("ks_tok", [NS, 256], F32)
    vs_tok = P.sb("vs_tok", [NS, 256], F32)
    ident = P.sb("ident_s", [128, 128], F32)
    identb = P.sb("identb", [128, 128], BF16)
    ones_mean = P.sb("ones_mean", [128, 128], BF16)
    blockones = P.sb("blockones", [128, 128], BF16)
    ones_bf = P.sb("ones_bf", [128, 128], BF16)
    smalls = P.sb("smalls_s", [128, NSM], F32)
    esink = P.sb("esink", [128, 16], F32)
    gsg = P.sb("gsg", [128, 4], F32)
    eps_t = P.sb("eps_t", [128, 1], F32)
    PTs = [P.sb("PTs%d" % i, [128, 2 * NS], BF16) for i in range(2)]

    banks = [P.psum("bank%d" % i, [128, 512], F32) for i in range(7)]
    bankb = P.psum("bankb", [128, 1024], BF16)

    G1, GM, G2, GKV, CW = 0, 32, 64, 96, 104
    GQ, GQS, GK, GKS, FQ, SGN, SINK = 152, 154, 156, 157, 158, 159, 160

    def sm(c):
        return smalls[:, c:c + 1]

    def XN(ti, ds=range(NCH)):
        return ["x_%d_%d" % (ti, d) for d in ds]

    def HN(ti):
        return ["h_%d_%d" % (ti, k) for k in range(NCH)]

    dma(SP, smalls[:], smallsd, writes=["smalls"], track="smalls")
    dma(SP, ident[:], identd, writes=["ident"], track="ident")
    dma(SP, xs[0][:, 0:512], mask4d, writes=["xs0"], track="xs0")
    op(DVE, lambda: nc.vector.tensor_copy(out=mask4[:], in_=xs[0][:, 0:512]), reads=["xs0"], writes=["mask4"])
    op(DVE, lambda: nc.vector.tensor_copy(out=identb[:], in_=ident[:]), reads=["ident"], writes=["identb"])
    op(DVE, lambda: nc.vector.memset(ones_mean[:], 1.0 / D), writes=["ones_mean"])
    op(DVE, lambda: nc.vector.memset(ones_bf[:], 1.0), writes=["ones_bf"])
    op(DVE, lambda: nc.vector.memset(blockones[:], 0.0), writes=["blockones"])
    op(DVE, lambda: nc.vector.memset(blockones[0:64, 0:64], 1.0 / 64), writes=["blockones"])
    op(DVE, lambda: nc.vector.memset(blockones[64:128, 64:128], 1.0 / 64), writes=["blockones"])
    op(DVE, lambda: nc.vector.memset(eps_t[:], EPS), writes=["eps"])
    op(DVE, lambda: nc.vector.memset(cu[0][:], 0.0), writes=["cu0"])
    op(DVE, lambda: nc.vector.memset(cu[1][:], 0.0), writes=["cu1"])
    op(DVE, lambda: nc.vector.tensor_scalar(out=gsg[:, 0:2], in0=smalls[:, GQS:GQS + 2], scalar1=sm(SGN), scalar2=None, op0=ALU.mult),
       reads=["smalls"], writes=["gsg"])
    op(DVE, lambda: nc.vector.tensor_scalar(out=gsg[:, 2:3], in0=smalls[:, GKS:GKS + 1], scalar1=sm(SGN), scalar2=None, op0=ALU.mult),
       reads=["smalls"], writes=["gsg"])
    op(ACT, lambda: nc.scalar.activation(out=esink[:], in_=smalls[:, SINK:SINK + 16], func=AF.Exp), reads=["smalls"], writes=["esink"])

    groups = []

    def slotres(s):
        return P.R("wslot%d" % s)

    def wload(s, *pairs):
        P.dma_batch(POOL, list(pairs), writes=[slotres(s)], track=slotres(s))

    def norm_tile_a(ti, c0, n):
        op(ACT, lambda: nc.scalar.activation(out=sq[:, :, 0:n], in_=x_sb[:, :, c0:c0 + n], func=AF.Square),
           reads=XN(ti), writes=["sq"])

    def norm_tile_b1(ti, c0, n):
        bk = banks[6]

        def mm():
            for k in range(NCH):
                i = nc.tensor.matmul(bk[:, 0:n], lhsT=ones_mean[:], rhs=sq[:, k, 0:n], start=(k == 0), stop=(k == NCH - 1))
            return i
        op(PE, mm, reads=["sq", "ones_mean"], writes=["bank6"])
        rs = rstd[ti % 2]
        rsn = "rstd%d" % (ti % 2)
        op(ACT, lambda: nc.scalar.activation(out=lnt[:, 0:n], in_=bk[:, 0:n], func=AF.Ln, bias=eps_t[:, 0:1]),
           reads=["bank6", "eps"], writes=["lnt"])
        op(ACT, lambda: nc.scalar.activation(out=rs[:, 0:n], in_=lnt[:, 0:n], func=AF.Exp, scale=-0.5),
           reads=["lnt"], writes=[rsn])

    def norm_tile_b2(ti, c0, n, gcol):
        xr = "x_%d" % ti
        rs = rstd[ti % 2]
        rsn = "rstd%d" % (ti % 2)
        return [(lambda k=k: op(DVE, lambda: nc.vector.scalar_tensor_tensor(out=h_sb[:, k, c0:c0 + n], in0=x_sb[:, k, c0:c0 + n], scalar=sm(gcol + k),
                                                                            in1=rs[:, 0:n], op0=ALU.mult, op1=ALU.mult),
                                reads=["x_%d_%d" % (ti, k), rsn, "smalls"], writes=["h_%d_%d" % (ti, k)])) for k in range(NCH)]

    def norm_tile_b(ti, c0, n, gcol):
        norm_tile_b1(ti, c0, n)
        for f in norm_tile_b2(ti, c0, n, gcol):
            f()

    def norm_tile(ti, c0, n, gcol):
        norm_tile_a(ti, c0, n)
        norm_tile_b(ti, c0, n, gcol)

    def norm_hook(ti, c0, n, gcol):
        norm_tile_a(ti, c0, n)

        def b1():
            norm_tile_b1(ti, c0, n)
            P.trickle.extend(norm_tile_b2(ti, c0, n, gcol))
        P.after_gu.append(b1)

    def norm_stage(tiles, gcol):
        if P.skip_norm:
            P.skip_norm = False
            return
        P.drain()
        for ti, (c0, n) in enumerate(tiles):
            norm_tile(ti, c0, n, gcol)

    def ffn_groups(l, which, tiles):
        wg, wu, wd = (w1g, w1u, w1d) if which == 1 else (w2g, w2u, w2d)
        gcol = (G1 if which == 1 else G2) + 8 * l
        out = []
        for gi, (fc0, G) in enumerate(FFN_GROUPS):
            def load(s, fc0=fc0, G=G):
                wload(s, (wA[s][:, :, 0:G * 128], wg[l].rearrange("(k p) n -> p k n", p=128)[:, :, fc0 * 128:(fc0 + G) * 128]),
                      (wA[s][:, :, 512:512 + G * 128], wu[l].rearrange("(k p) n -> p k n", p=128)[:, :, fc0 * 128:(fc0 + G) * 128]),
                      (wB[s][:, 0:G, :], wd[l][fc0 * 128:(fc0 + G) * 128, :].rearrange("(j p) n -> p j n", p=128)))

            def compute(s, gi=gi, G=G):
                hook = P.hook
                if gi == 0:
                    norm_stage(tiles, gcol)
                for ti, (c0, n) in enumerate(tiles):
                    P.ucount += 1
                    ab = P.ucount % 2

                    def gu(ti=ti, c0=c0, n=n, ab=ab):
                        for j in range(G):
                            pg = banks[j % 2]; pu = banks[2 + j % 2]
                            pgn = "bank%d" % (j % 2); pun = "bank%d" % (2 + j % 2)

                            def mmg(j=j, pg=pg):
                                for k in range(NCH):
                                    i = nc.tensor.matmul(pg[:, 0:n], lhsT=wA[s][:, k, j * 128:(j + 1) * 128], rhs=h_sb[:, k, c0:c0 + n],
                                                         start=(k == 0), stop=(k == NCH - 1))
                                return i

                            def mmu(j=j, pu=pu):
                                for k in range(NCH):
                                    i = nc.tensor.matmul(pu[:, 0:n], lhsT=wA[s][:, k, 512 + j * 128:512 + (j + 1) * 128], rhs=h_sb[:, k, c0:c0 + n],
                                                         start=(k == 0), stop=(k == NCH - 1))
                                return i
                            op(PE, mmg, reads=[slotres(s)] + HN(ti), writes=[pgn])
                            op(PE, mmu, reads=[slotres(s)] + HN(ti), writes=[pun])
                            sgj = sg[j % 2]; sgn_ = "sg%d" % (j % 2)
                            op(ACT, lambda pg=pg, sgj=sgj: nc.scalar.activation(out=sgj[:, 0:n], in_=pg[:, 0:n], func=AF.Silu),
                               reads=[pgn], writes=[sgn_])
                            op(DVE, lambda j=j, pu=pu, sgj=sgj: nc.vector.tensor_tensor(out=act[ab][:, j, 0:n], in0=pu[:, 0:n], in1=sgj[:, 0:n], op=ALU.mult),
                               reads=[pun, sgn_], writes=["act%d_%d" % (ab, j)])
                            if j == 0:
                                P.run_after_gu()
                            else:
                                P.run_trickle(3)

                    def dn(ti=ti, c0=c0, n=n, ab=ab):
                        for d in range(NCH):
                            pd = banks[4 + d % 3]; pdn = "bank%d" % (4 + d % 3)

                            def mmd(d=d, pd=pd):
                                for j in range(G):
                                    i = nc.tensor.matmul(pd[:, 0:n], lhsT=wB[s][:, j, d * 128:(d + 1) * 128], rhs=act[ab][:, j, 0:n],
                                                         start=(j == 0), stop=(j == G - 1))
                                return i
                            op(PE, mmd, reads=[slotres(s)] + ["act%d_%d" % (ab, j) for j in range(G)], writes=[pdn])
                            op(DVE, lambda d=d, pd=pd: nc.vector.scalar_tensor_tensor(out=x_sb[:, d, c0:c0 + n], in0=pd[:, 0:n], scalar=0.5,
                                                                                     in1=x_sb[:, d, c0:c0 + n], op0=ALU.mult, op1=ALU.add),
                               reads=[pdn], writes=["x_%d_%d" % (ti, d)])
                        if hook is not None:
                            hook(ti)
                    P.unit(gu, dn)
            out.append((load, compute))
        return out

    def mixA_groups(l, ps, MT):
        out = []
        for gi in range(4):
            def load(s, gi=gi):
                prs = [(wA[s][:, :, part * 256:(part + 1) * 256],
                        wind[l].rearrange("(k p) n -> p k n", p=128)[:, :, part * D + gi * 256: part * D + (gi + 1) * 256]) for part in range(3)]
                prs.append((wB[s][:, 0:2, :], woutd[l][gi * 256:(gi + 1) * 256, :].rearrange("(j p) n -> p j n", p=128)))
                wload(s, *prs)

            def compute(s, gi=gi):
                hook = P.hook
                if gi == 0:
                    norm_stage(MT, GM + 8 * l)
                cui = [0]
                for ti, (c0, n) in enumerate(MT):
                    P.ucount += 1
                    zi = P.ucount % 2

                    def gu(ti=ti, c0=c0, n=n, zi=zi):
                        for jj in range(2):
                            j = gi * 2 + jj
                            pb, pc, pu = banks[jj], banks[2], banks[3]
                            pbn, pcn, pun = "bank%d" % jj, "bank2", "bank3"

                            def mm(part, bk, jj=jj):
                                def f():
                                    for k in range(NCH):
                                        i = nc.tensor.matmul(bk[:, 0:n], lhsT=wA[s][:, k, part * 256 + jj * 128: part * 256 + (jj + 1) * 128],
                                                             rhs=h_sb[:, k, c0:c0 + n], start=(k == 0), stop=(k == NCH - 1))
                                    return i
                                return f
                            op(PE, mm(1, pc), reads=[slotres(s)] + HN(ti), writes=[pcn])
                            op(PE, mm(2, pu), reads=[slotres(s)] + HN(ti), writes=[pun])
                            op(PE, mm(0, pb), reads=[slotres(s)] + HN(ti), writes=[pbn])
                            cb = cu[jj]; cbn = "cu%d" % jj
                            op(ACT, lambda pc=pc: nc.scalar.copy(out=c_sb[:, 0:n], in_=pc[:, 0:n]), reads=[pcn], writes=["c_sb"])
                            if ti == 0 and ps == 0:
                                op(DVE, lambda cb=cb: nc.vector.memset(cb[:, 0:2], 0.0), writes=[cbn])
                            elif ti == 0:
                                op(DVE, lambda cb=cb, j=j: nc.vector.tensor_copy(out=cb[:, 0:2], in_=sv[:, l, j, NS:NS + 2]), reads=["sv"], writes=[cbn])
                            else:
                                pn = MT[ti - 1][1]
                                op(DVE, lambda cb=cb, pn=pn: nc.vector.tensor_copy(out=cb[:, 0:2], in_=cb[:, pn:pn + 2]), reads=[cbn], writes=[cbn])
                            op(DVE, lambda cb=cb, pu=pu: nc.vector.tensor_tensor(out=cb[:, 2:2 + n], in0=pu[:, 0:n], in1=c_sb[:, 0:n], op=ALU.mult),
                               reads=[pun, "c_sb"], writes=[cbn])
                            cw0, cw1, cw2 = (sm(CW + l * 24 + tap * 8 + j) for tap in range(3))
                            op(ACT, lambda cb=cb, cw0=cw0: nc.scalar.mul(out=conv[:, 0:n], in_=cb[:, 0:n], mul=cw0),
                               reads=[cbn, "smalls"], writes=["conv"])
                            op(DVE, lambda cb=cb, cw1=cw1: nc.vector.scalar_tensor_tensor(out=conv[:, 0:n], in0=cb[:, 1:1 + n], scalar=cw1, in1=conv[:, 0:n],
                                                                                         op0=ALU.mult, op1=ALU.add), reads=[cbn, "conv", "smalls"], writes=["conv"])
                            op(DVE, lambda cb=cb, cw2=cw2: nc.vector.scalar_tensor_tensor(out=conv[:, 0:n], in0=cb[:, 2:2 + n], scalar=cw2, in1=conv[:, 0:n],
                                                                                         op0=ALU.mult, op1=ALU.add), reads=[cbn, "conv", "smalls"], writes=["conv"])
                            if ti == len(MT) - 1:
                                so = n - NS
                                stv = st_fm[:, l, j, :].rearrange("p (s r) -> p s r", r=2)
                                op(DVE, lambda cb=cb, cw2=cw2, so=so: nc.vector.tensor_scalar(out=conv[:, so:n], in0=cb[:, 2 + so:2 + n], scalar1=cw2, scalar2=None, op0=ALU.mult),
                                   reads=[cbn, "smalls"], writes=["conv"])
                                op(DVE, lambda stv=stv, cw1=cw1, so=so: nc.vector.scalar_tensor_tensor(out=conv[:, so:n], in0=stv[:, :, 1], scalar=cw1, in1=conv[:, so:n],
                                                                                                      op0=ALU.mult, op1=ALU.add), reads=["st_fm", "conv", "smalls"], writes=["conv"])
                                op(DVE, lambda stv=stv, cw0=cw0, so=so: nc.vector.scalar_tensor_tensor(out=conv[:, so:n], in0=stv[:, :, 0], scalar=cw0, in1=conv[:, so:n],
                                                                                                      op0=ALU.mult, op1=ALU.add), reads=["st_fm", "conv", "smalls"], writes=["conv"])
                                op(ACT, lambda cb=cb, so=so: nc.scalar.copy(out=sv[:, l, j, 0:NS], in_=cb[:, 2 + so:2 + n]), reads=[cbn], writes=["sv"])
                                op(ACT, lambda cb=cb, so=so: nc.scalar.copy(out=sv[:, l, j, NS:NS + 2], in_=cb[:, so:so + 2]), reads=[cbn], writes=["sv"])
                            op(DVE, lambda pb=pb, jj=jj: nc.vector.tensor_tensor(out=zb[zi][:, jj, 0:n], in0=pb[:, 0:n], in1=conv[:, 0:n], op=ALU.mult),
                               reads=[pbn, "conv"], writes=["zb%d_%d" % (zi, jj)])
                            if jj == 0:
                                P.run_after_gu()
                            else:
                                P.run_trickle(4)

                    def dn(ti=ti, c0=c0, n=n, zi=zi):
                        for d in range(NCH):
                            pd = banks[4 + d % 3]; pdn = "bank%d" % (4 + d % 3)

                            def mmd(d=d, pd=pd):
                                for jj in range(2):
                                    i = nc.tensor.matmul(pd[:, 0:n], lhsT=wB[s][:, jj, d * 128:(d + 1) * 128], rhs=zb[zi][:, jj, 0:n],
                                                         start=(jj == 0), stop=(jj == 1))
                                return i
                            op(PE, mmd, reads=[slotres(s), "zb%d_0" % zi, "zb%d_1" % zi], writes=[pdn])
                            op(DVE, lambda d=d, pd=pd: nc.vector.tensor_tensor(out=x_sb[:, d, c0:c0 + n], in0=pd[:, 0:n], in1=x_sb[:, d, c0:c0 + n], op=ALU.add),
                               reads=[pdn], writes=["x_%d_%d" % (ti, d)])
                        if hook is not None:
                            hook(ti)
                    P.unit(gu, dn)
            out.append((load, compute))
        return out

    def qk_post_a(pq, pqn, n, sqi):
        op(ACT, lambda: nc.scalar.activation(out=qsq[:, sqi, 0:n], in_=pq[:, 0:n], func=AF.Square), reads=[pqn], writes=["qsq_%d" % sqi])

    def qk_post_b(pq, pqn, pq2, pq2n, c0, n, gcol_ap, gs_ap, outs, sqi):
        bk = banks[6]
        op(PE, lambda: nc.tensor.matmul(bk[:, 0:n], lhsT=blockones[:], rhs=qsq[:, sqi, 0:n], start=True, stop=True),
           reads=["qsq_%d" % sqi, "blockones"], writes=["bank6"])
        op(ACT, lambda: nc.scalar.activation(out=lnt[:, 0:n], in_=bk[:, 0:n], func=AF.Ln, bias=eps_t[:, 0:1]), reads=["bank6", "eps"], writes=["lnt"])
        op(ACT, lambda: nc.scalar.activation(out=qrs[:, 0:n], in_=lnt[:, 0:n], func=AF.Exp, scale=-0.5), reads=["lnt"], writes=["qrs"])
        op(DVE, lambda: nc.vector.scalar_tensor_tensor(out=t1[:, 0:n], in0=pq[:, 0:n], scalar=gcol_ap, in1=cosT[:, c0:c0 + n], op0=ALU.mult, op1=ALU.mult),
           reads=[pqn, "cosT", "smalls"], writes=["t1"])
        op(DVE, lambda: nc.vector.scalar_tensor_tensor(out=t2[:, 0:n], in0=pq2[:, 0:n], scalar=gs_ap, in1=sinT[:, c0:c0 + n], op0=ALU.mult, op1=ALU.mult),
           reads=[pq2n, "sinT", "gsg"], writes=["t2"])
        op(DVE, lambda: nc.vector.tensor_tensor(out=t1[:, 0:n], in0=t1[:, 0:n], in1=t2[:, 0:n], op=ALU.add), reads=["t1", "t2"], writes=["t1"])
        for (oap, oname, a, b) in outs:
            op(DVE, lambda oap=oap, a=a, b=b: nc.vector.tensor_tensor(out=oap, in0=t1[:, a:b], in1=qrs[:, a:b], op=ALU.mult),
               reads=["t1", "qrs"], writes=[oname])

    def qk_post(pq, pqn, pq2, pq2n, c0, n, gcol_ap, gs_ap, outs):
        qk_post_a(pq, pqn, n, 0)
        qk_post_b(pq, pqn, pq2, pq2n, c0, n, gcol_ap, gs_ap, outs, 0)

    def kv_group(ps, KT_):
        def load(s):
            wload(s, (wA[s][:, :, 0:256], wkvd.rearrange("(k p) n -> p k n", p=128)[:, :, 0:256]),
                  (wA[s][:, :, 256:512], wksd.rearrange("(k p) n -> p k n", p=128)),
                  (wA[s][:, :, 512:768], wkvd.rearrange("(k p) n -> p k n", p=128)[:, :, 256:512]))

        def compute(s):
            norm_stage(KT_, GKV)
            if ps == 1:
                op(DVE, lambda: nc.vector.tensor_copy(out=kT[:, :, 4:132], in_=kT[:, :, HALO + OWN - 128:HALO + OWN]), reads=["kT"], writes=["kT"])
                op(DVE, lambda: nc.vector.tensor_copy(out=V_sb[:, 0, :], in_=V_sb[:, 8, :]), reads=["V_sb"], writes=["V_sb"])
            if 'kv1' in DBG:
                return
            for ti, (c0, n) in enumerate(KT_):
                for m in range(2):
                    pk, pk2 = banks[m], banks[2 + m]
                    pkn, pk2n = "bank%d" % m, "bank%d" % (2 + m)

                    def mm(off, bk, m=m):
                        def f():
                            for k in range(NCH):
                                i = nc.tensor.matmul(bk[:, 0:n], lhsT=wA[s][:, k, off + m * 128: off + (m + 1) * 128], rhs=h_sb[:, k, c0:c0 + n],
                                                     start=(k == 0), stop=(k == NCH - 1))
                            return i
                        return f
                    op(PE, mm(0, pk), reads=[slotres(s)] + HN(ti), writes=[pkn])
                    op(PE, mm(256, pk2), reads=[slotres(s)] + HN(ti), writes=[pk2n])
                    outs = [(kT[:, m, c0:c0 + n], "kT", 0, n)]
                    if ti == 2:
                        a = (HALO + OWN - 128) - c0
                        outs.append((ktail[:, m, 0:136], "ktail", a, a + 136))
                    qk_post(pk, pkn, pk2, pk2n, c0, n, sm(GK), gsg[:, 2:3], outs)
            if 'kv2' in DBG:
                return
            for b in range(0 if ps == 0 else 1, 9):
                kc0 = 4 + 128 * b
                pv = banks[4 + b % 2]; pvn = "bank%d" % (4 + b % 2)
                hres = [nm for ti, (c0, n) in enumerate(KT_) if c0 < kc0 + 128 and kc0 < c0 + n for nm in HN(ti)]

                def mmv(kc0=kc0, pv=pv):
                    for k in range(NCH):
                        i = nc.tensor.matmul(pv[:, 0:256], lhsT=h_sb[:, k, kc0:kc0 + 128], rhs=wA[s][:, k, 512:768], start=(k == 0), stop=(k == NCH - 1))
                    return i
                op(PE, mmv, reads=[slotres(s)] + hres, writes=[pvn])
                op(ACT, lambda b=b, pv=pv: nc.scalar.copy(out=V_sb[:, b, :], in_=pv[:, 0:256]), reads=[pvn], writes=["V_sb"])
                if b == 8:
                    op(DVE, lambda pv=pv: nc.vector.tensor_copy(out=vp_tok[:], in_=pv[:, 0:256]), reads=[pvn], writes=["vp_tok"])
                    dma(SP, v_p_o[ps], vp_tok[:], reads=["vp_tok"], track="vp_tok")
            sc0 = HALO + OWN
            if 'no_vs' in DBG:
                return
            pv = banks[4]

            def mmvs():
                for k in range(NCH):
                    i = nc.tensor.matmul(pv[0:NS, 0:256], lhsT=h_sb[:, k, sc0:sc0 + NS], rhs=wA[s][:, k, 512:768], start=(k == 0), stop=(k == NCH - 1))
                return i
            op(PE, mmvs, reads=[slotres(s)] + HN(2), writes=["bank4"])
            op(DVE, lambda: nc.vector.tensor_copy(out=vs_tok[:], in_=pv[0:NS, 0:256]), reads=["bank4"], writes=["vs_tok"])
            dma(SP, v_s_o[ps, :, 127, :], vs_tok[:], reads=["vs_tok"], writes=["vso"], track="vso")
            pt = banks[5]
            if 'no_ktail' in DBG:
                return

            def trk():
                for m in range(2):
                    i = nc.tensor.transpose(out=pt[:, m * 128:(m + 1) * 128], in_=ktail[:, m, 0:128], identity=ident[:])
                return i
            op(PE, trk, reads=["ktail", "ident"], writes=["bank5"])
            op(DVE, lambda: nc.vector.tensor_copy(out=kp_tok[:], in_=pt[:, 0:256]), reads=["bank5"], writes=["kp_tok"])
            dma(SP, k_p_o[ps], kp_tok[:], reads=["kp_tok"], track="kp_tok")

            def trks():
                for m in range(2):
                    i = nc.tensor.transpose(out=pt[0:NS, 256 + m * 128:256 + (m + 1) * 128], in_=ktail[:, m, 128:136], identity=ident[:])
                return i
            op(PE, trks, reads=["ktail", "ident"], writes=["bank5"])
            op(DVE, lambda: nc.vector.tensor_copy(out=ks_tok[:], in_=pt[0:NS, 256:512]), reads=["bank5"], writes=["ks_tok"])
            dma(SP, k_s_o[ps, :, 127, :], ks_tok[:], reads=["ks_tok"], writes=["kso"], track="kso")
            if 'no_reload' in DBG:
                return
            dma(POOL, Kc[:], k_s_o[ps].rearrange("s k d -> k s d"), reads=["kso"], writes=["Kc"], track="Kc")
            dma(POOL, Vc[:], v_s_o[ps].rearrange("s k d -> k s d"), reads=["vso"], writes=["Vc"], track="Vc")
            for si in range(NS):
                def trc(si=si):
                    for m in range(2):
                        i = nc.tensor.transpose(out=bankb[:, m * 128:(m + 1) * 128], in_=Kc[:, si, m * 128:(m + 1) * 128], identity=identb[:])
                    return i
                op(PE, trc, reads=["Kc", "identb"], writes=["bankb"])
                op(ACT, lambda si=si: nc.scalar.copy(out=KTa[:, :, si, :], in_=bankb[:, 0:256].rearrange("p (m k) -> p m k", m=2)),
                   reads=["bankb"], writes=["KTa"])
        return [(load, compute)]

    def mixB_groups(jl, l, ps):
        out = []
        for gi in range(4):
            def load(s, gi=gi):
                wload(s, (wA[s][:, :, 0:256], wqpd[jl].rearrange("(k p) n -> p k n", p=128)[:, :, gi * 256:(gi + 1) * 256]),
                      (wA[s][:, :, 256:512], wqsd[jl].rearrange("(k p) n -> p k n", p=128)[:, :, gi * 256:(gi + 1) * 256]),
                      (wB[s][:, 0:2, :], wopd[jl][gi * 256:(gi + 1) * 256, :].rearrange("(j p) n -> p j n", p=128)))

            def compute(s, gi=gi):
                hook = P.hook
                if gi == 0:
                    norm_stage(B_TILES, GM + 8 * l)
                    P.flush()
                par = P.bcount % 2
                P.bcount += 1
                obs = [oT[par * 2 + cc] for cc in range(2)]
                obns = ["oT%d" % (par * 2 + cc) for cc in range(2)]

                def qproj2(sq_):
                    prev = None
                    step = 0
                    for ti, (c0, n) in enumerate(B_TILES):
                        for cc in range(2):
                            qb = qn[cc]; qbn = "qn%d" % cc
                            if step == 1:
                                P.run_after_gu()
                            elif step > 1:
                                P.run_trickle(4)
                            bi = (step % 2) * 2
                            pq, pq2 = banks[bi], banks[bi + 1]
                            pqn, pq2n = "bank%d" % bi, "bank%d" % (bi + 1)

                            def mm(off, bk, c0=c0, n=n, cc=cc):
                                def f():
                                    for k in range(NCH):
                                        i = nc.tensor.matmul(bk[:, 0:n], lhsT=wA[sq_][:, k, off + cc * 128: off + (cc + 1) * 128], rhs=h_sb[:, k, c0:c0 + n],
                                                             start=(k == 0), stop=(k == NCH - 1))
                                    return i
                                return f
                            op(PE, mm(0, pq), reads=[slotres(sq_)] + HN(ti), writes=[pqn])
                            op(PE, mm(256, pq2), reads=[slotres(sq_)] + HN(ti), writes=[pq2n])
                            qk_post_a(pq, pqn, n, step % 2)
                            if prev is not None:
                                prev()
                            prev = (lambda pq=pq, pqn=pqn, pq2=pq2, pq2n=pq2n, c0=c0, n=n, qb=qb, qbn=qbn, sqi=step % 2:
                                    qk_post_b(pq, pqn, pq2, pq2n, c0, n, sm(GQ + jl), gsg[:, jl:jl + 1], [(qb[:, c0 - HALO:c0 - HALO + n], qbn, 0, n)], sqi))
                            step += 1
                    prev()

                def attention2():
                    cs = [gi * 2 + cc for cc in range(2)]
                    ms = [c // 4 for c in cs]
                    qbs = [qn[cc] for cc in range(2)]; qbns = ["qn%d" % cc for cc in range(2)]
                    dalls = [xs[cc] for cc in range(2)]; dallns = ["xs%d" % cc for cc in range(2)]
                    ess = [esink[:, jl * 8 + c: jl * 8 + c + 1] for c in cs]

                    def st(cc, j):
                        q0 = 128 * j
                        m = ms[cc]; qb = qbs[cc]; qbn = qbns[cc]
                        pt = PT[cc * 2 + j % 2]; ptn = "PT%d" % (cc * 2 + j % 2)
                        for hd in range(2):
                            bk = banks[2 * cc + hd]; bkn = "bank%d" % (2 * cc + hd)

                            def f(hd=hd, bk=bk):
                                for kb in range(2):
                                    kc = 4 + 128 * (j + kb)
                                    i = nc.tensor.matmul(bk[:, kb * 128:(kb + 1) * 128], lhsT=kT[hd * 64:(hd + 1) * 64, m, kc:kc + 128],
                                                         rhs=qb[hd * 64:(hd + 1) * 64, q0:q0 + 128], start=True, stop=True)
                                return i
                            op(PE, f, reads=["kT", qbn], writes=[bkn])
                            op(ACT, lambda hd=hd, bk=bk: nc.scalar.activation(out=pt[:, hd * 256:(hd + 1) * 256], in_=bk[:, 0:256], func=AF.Exp, scale=0.125),
                               reads=[bkn], writes=[ptn + "_h%d" % hd])
                        mk = mask40 if j == 0 else mask4
                        op(DVE, lambda: nc.vector.tensor_tensor(out=pt[:], in0=pt[:], in1=mk[:], op=ALU.mult), reads=["mask4", "mask40"], writes=[ptn + "_h0", ptn + "_h1"])

                    def pv(cc, j):
                        m = ms[cc]; ob = obs[cc]; obn = obns[cc]; dall = dalls[cc]; dalln = dallns[cc]
                        bk = banks[4 + cc]; bkn = "bank%d" % (4 + cc)
                        pt = PT[cc * 2 + j % 2]; ptn = "PT%d" % (cc * 2 + j % 2)
                        q0 = 128 * j

                        def f():
                            for hd in range(2):
                                for kb in range(2):
                                    i = nc.tensor.matmul(bk[hd * 64:(hd + 1) * 64, 0:128], lhsT=V_sb[:, j + kb, m * 128 + hd * 64: m * 128 + (hd + 1) * 64],
                                                         rhs=pt[:, (hd * 2 + kb) * 128:(hd * 2 + kb + 1) * 128], start=(kb == 0), stop=(kb == 1))
                            for hd in range(2):
                                for kb in range(2):
                                    i = nc.tensor.matmul(bk[hd * 64:(hd + 1) * 64, 128:256], lhsT=ones_bf[:, 0:64],
                                                         rhs=pt[:, (hd * 2 + kb) * 128:(hd * 2 + kb + 1) * 128], start=(kb == 0), stop=(kb == 1))
                            return i
                        op(PE, f, reads=["V_sb", ptn + "_h0", ptn + "_h1", "ones_bf"], writes=[bkn])
                        op(ACT, lambda: nc.scalar.copy(out=ob[:, q0:q0 + 128], in_=bk[:, 0:128]), reads=[bkn], writes=[obn + "_b%d" % j])
                        op(DVE, lambda: nc.vector.tensor_scalar(out=dall[:, q0:q0 + 128], in0=bk[:, 128:256], scalar1=ess[cc][:, 0:1], scalar2=None, op0=ALU.add),
                           reads=[bkn, "esink"], writes=[dalln + "_b%d" % j])

                    def sample(cc):
                        m = ms[cc]; qb = qbs[cc]; qbn = qbns[cc]; ob = obs[cc]; obn = obns[cc]; dall = dalls[cc]; dalln = dallns[cc]
                        pts = PTs[cc]; ptsn = "PTs%d" % cc
                        for hd in range(2):
                            bk = banks[2 * cc + hd]; bkn = "bank%d" % (2 * cc + hd)

                            def fs(hd=hd, bk=bk):
                                for si in range(NS):
                                    i = nc.tensor.matmul(bk[:, si:si + 1], lhsT=KTa[hd * 64:(hd + 1) * 64, m, si, :],
                                                         rhs=qb[hd * 64:(hd + 1) * 64, OWN + si:OWN + si + 1], start=True, stop=True)
                                return i
                            op(PE, fs, reads=["KTa", qbn], writes=[bkn])
                            op(ACT, lambda hd=hd, bk=bk: nc.scalar.activation(out=pts[:, hd * NS:(hd + 1) * NS], in_=bk[:, 0:NS], func=AF.Exp, scale=0.125),
                               reads=[bkn], writes=[ptsn])
                        bo = banks[4 + cc]; bon = "bank%d" % (4 + cc)

                        def fo():
                            for si in range(NS):
                                for hd in range(2):
                                    i = nc.tensor.matmul(bo[hd * 64:(hd + 1) * 64, si:si + 1], lhsT=Vc[:, si, m * 128 + hd * 64: m * 128 + (hd + 1) * 64],
                                                         rhs=pts[:, hd * NS + si:hd * NS + si + 1], start=True, stop=True)
                            i = nc.tensor.matmul(bo[:, 128:128 + 2 * NS], lhsT=ones_bf[:], rhs=pts[:], start=True, stop=True)
                            return i
                        op(PE, fo, reads=["Vc", ptsn, "ones_bf"], writes=[bon])
                        op(DVE, lambda: nc.vector.tensor_copy(out=ob[:, OWN:OWN + NS], in_=bo[:, 0:NS]), reads=[bon], writes=[obn + "_b8"])
                        for hd in range(2):
                            pr = slice(hd * 64, (hd + 1) * 64)
                            op(DVE, lambda pr=pr, hd=hd: nc.vector.tensor_scalar(out=dall[pr, OWN:OWN + NS], in0=bo[pr, 128 + hd * NS:128 + (hd + 1) * NS],
                                                                               scalar1=ess[cc][pr, 0:1], scalar2=None, op0=ALU.add),
                               reads=[bon, "esink"], writes=[dalln + "_b8"])

                    def finalize(cc):
                        ob = obs[cc]; obn = obns[cc]; dall = dalls[cc]; dalln = dallns[cc]; es_ap = ess[cc]
                        W = OWN + NS
                        dbl = [dalln + "_b%d" % b for b in range(9)]
                        obl = [obn + "_b%d" % b for b in range(9)]
                        op(ACT, lambda: nc.scalar.activation(out=dall[:, 0:W], in_=dall[:, 0:W], func=AF.Ln), writes=[dalln] + dbl)
                        op(ACT, lambda: nc.scalar.activation(out=dall[:, 0:W], in_=dall[:, 0:W], func=AF.Exp, scale=-1.0), writes=[dalln] + dbl)
                        op(DVE, lambda: nc.vector.tensor_tensor(out=ob[:, 0:W], in0=ob[:, 0:W], in1=dall[:, 0:W], op=ALU.mult), reads=[dalln], writes=[obn] + obl)

                    st(0, 0)
                    st(1, 0)
                    for j in range(8):
                        if j + 1 < 8:
                            st(0, j + 1)
                            st(1, j + 1)
                        pv(0, j)
                        pv(1, j)
                    sample(0)
                    sample(1)
                    finalize(0)
                    finalize(1)

                def dn():
                    for ti, (c0, n) in enumerate(B_TILES):
                        for d in range(NCH):
                            pd = banks[4 + d % 3]; pdn = "bank%d" % (4 + d % 3)

                            def mmd(d=d, pd=pd, c0=c0, n=n):
                                for cc in range(2):
                                    i = nc.tensor.matmul(pd[:, 0:n], lhsT=wB[s][:, cc, d * 128:(d + 1) * 128], rhs=obs[cc][:, c0 - HALO:c0 - HALO + n],
                                                         start=(cc == 0), stop=(cc == 1))
                                return i
                            op(PE, mmd, reads=[slotres(s)] + obns, writes=[pdn])
                            op(DVE, lambda d=d, pd=pd, c0=c0, n=n: nc.vector.tensor_tensor(out=x_sb[:, d, c0:c0 + n], in0=pd[:, 0:n], in1=x_sb[:, d, c0:c0 + n], op=ALU.add),
                               reads=[pdn], writes=["x_%d_%d" % (ti, d)])
                            P.run_trickle(1)
                        P.run_after_gu()
                        if hook is not None:
                            hook(ti)
                if gi == 0:
                    qproj2(s)
                attention2()
                P.flush()
                if gi < 3:
                    qproj2(1 - s)
                dn()
            out.append((load, compute))
        return out

    def prologue(ps):
        P.drain()
        P.fence()
        dma(SP, k_s_o[ps, :, 0:127, :], ckd[ps, :, 1:128, :], writes=["kso"], track="kso")
        dma(SP, v_s_o[ps, :, 0:127, :], cvd[ps, :, 1:128, :], writes=["vso"], track="vso")
        for l in range(2):
            dma(SP, conv_s_o[ps, l, :, 0, :], stcd[ps, l].rearrange("(s r) d -> s r d", r=2)[:, 1, :], writes=["cso"], track="cso")
        dma(SP, xs[1][:, 0:512], mask40d[ps], writes=["xs1"], track="xs1")
        op(DVE, lambda: nc.vector.tensor_copy(out=mask40[:], in_=xs[1][:, 0:512]), reads=["xs1"], writes=["mask40"])
        for l in range(2):
            dma(SP, xs[l][0:2 * NS, 0:D], stcd[ps, l], writes=["xs%d" % l], track="xs%d" % l)
            bk = banks[l]

            def trs(l=l, bk=bk):
                for j in range(NCH):
                    i = nc.tensor.transpose(out=bk[:, j * 16:(j + 1) * 16], in_=xs[l][0:2 * NS, j * 128:(j + 1) * 128], identity=ident[0:2 * NS, 0:2 * NS])
                return i
            op(PE, trs, reads=["xs%d" % l, "ident"], writes=["bank%d" % l])
            op(DVE, lambda l=l, bk=bk: nc.vector.tensor_copy(out=st_fm[:, l, :, :], in_=bk[:, 0:128].rearrange("p (j c) -> p j c", j=NCH)),
               reads=["bank%d" % l], writes=["st_fm"])
        dma(SP, pos_sb[:], posd[ps], writes=["pos"], track="pos")
        for (tab, tname, off) in ((sinT, "sinT", 0.0), (cosT, "cosT", 0.25)):
            for (c0, n) in A_TILES:
                op(DVE, lambda c0=c0, n=n, off=off: nc.vector.tensor_scalar(out=t1[:, 0:n], in0=pos_sb[:, c0:c0 + n], scalar1=sm(FQ), scalar2=off, op0=ALU.mult, op1=ALU.add),
                   reads=["pos", "smalls"], writes=["t1"])
                op(DVE, lambda n=n: nc.vector.tensor_scalar(out=t2[:, 0:n], in0=t1[:, 0:n], scalar1=MAGIC, scalar2=MAGIC, op0=ALU.add, op1=ALU.subtract),
                   reads=["t1"], writes=["t2"])
                op(DVE, lambda n=n: nc.vector.tensor_tensor(out=t1[:, 0:n], in0=t1[:, 0:n], in1=t2[:, 0:n], op=ALU.subtract), reads=["t1", "t2"], writes=["t1"])
                op(ACT, lambda c0=c0, n=n, tab=tab: nc.scalar.activation(out=tab[:, c0:c0 + n], in_=t1[:, 0:n], func=AF.Sin, scale=TWO_PI_SAFE),
                   reads=["t1"], writes=[tname])
        nblk = (T + 127) // 128
        for b in range(nblk):
            r0 = b * 128
            nr = min(128, T - r0)
            if ps == 1 and r0 + nr <= HALO:
                continue
            xb_ = xs[b % 2]; xbn = "xs%d" % (b % 2)
            dma(SP, xb_[0:nr, 0:D], xin[ps, r0:r0 + nr, :], writes=[xbn], track=xbn)
            for half in range(2):
                bk = banks[2 + half]; bkn = "bank%d" % (2 + half)

                def trx(half=half, bk=bk, nr=nr, xb_=xb_):
                    for kk in range(4):
                        k = half * 4 + kk
                        i = nc.tensor.transpose(out=bk[:, kk * 128:kk * 128 + nr], in_=xb_[0:nr, k * 128:(k + 1) * 128], identity=ident[0:nr, 0:nr])
                    return i
                op(PE, trx, reads=[xbn, "ident"], writes=[bkn])
                xres = [nm for ti, (c0, n) in enumerate(A_TILES if ps == 0 else B_TILES) if c0 < r0 + nr and r0 < c0 + n
                        for nm in XN(ti, range(half * 4, half * 4 + 4))]
                src = bk[:].rearrange("p (k c) -> p k c", k=4)[:, :, 0:nr]
                if half == 0:
                    op(DVE, lambda src=src, r0=r0, nr=nr: nc.vector.tensor_copy(out=x_sb[:, 0:4, r0:r0 + nr], in_=src), reads=[bkn], writes=xres)
                else:
                    op(ACT, lambda src=src, r0=r0, nr=nr: nc.scalar.copy(out=x_sb[:, 4:8, r0:r0 + nr], in_=src), reads=[bkn], writes=xres)

    def epilogue(ps):
        P.drain()
        P.fence()
        blocks = [(HALO + 128 * b, 128, y_own[ps, 128 * b:128 * (b + 1), :]) for b in range(8)] + [(HALO + OWN, NS, y_s[ps])]
        for bi, (c0, nr, dst) in enumerate(blocks):
            stg = xs[bi % 2]; stn = "xs%d" % (bi % 2)
            for half in range(2):
                bk = banks[half]; bkn = "bank%d" % half

                def tro(half=half, bk=bk, c0=c0, nr=nr):
                    for kk in range(4):
                        k = half * 4 + kk
                        i = nc.tensor.transpose(out=bk[0:nr, kk * 128:(kk + 1) * 128], in_=x_sb[:, k, c0:c0 + nr], identity=ident[:])
                    return i
                op(PE, tro, reads=XN(0) + XN(1) + XN(2) + ["ident"], writes=[bkn])
                if half == 0:
                    op(DVE, lambda bk=bk, nr=nr, stg=stg: nc.vector.tensor_copy(out=stg[0:nr, 0:512], in_=bk[0:nr, :]), reads=[bkn], writes=[stn])
                else:
                    op(ACT, lambda bk=bk, nr=nr, stg=stg: nc.scalar.copy(out=stg[0:nr, 512:1024], in_=bk[0:nr, :]), reads=[bkn], writes=[stn])
            dma(SP, dst, stg[0:nr, 0:D], reads=[stn], track=stn)
        for l in range(2):
            stg = xs[l]; stn = "xs%d" % l
            for half in range(2):
                bk = banks[2 + half]; bkn = "bank%d" % (2 + half)

                def trv(half=half, bk=bk, l=l):
                    for kk in range(4):
                        j = half * 4 + kk
                        i = nc.tensor.transpose(out=bk[0:NS + 2, kk * 128:(kk + 1) * 128], in_=sv[:, l, j, :], identity=ident[:])
                    return i
                op(PE, trv, reads=["sv", "ident"], writes=[bkn])
                op(DVE, lambda bk=bk, half=half, stg=stg: nc.vector.tensor_copy(out=stg[0:NS + 2, half * 512:(half + 1) * 512], in_=bk[0:NS + 2, :]),
                   reads=[bkn], writes=[stn])
            P.dma_batch(SP, [(conv_s_o[ps, l, :, 1, :], stg[0:NS, 0:D]), (conv_p_o[ps, l], stg[NS:NS + 2, 0:D])], reads=[stn], writes=["cso"], track=stn)

    seq = []

    def add_stage(tag, groups, ps, tiles_id, gcol):
        for gi, (load, compute) in enumerate(groups):
            seq.append((tag, load, compute, ps, gi == 0, gi == len(groups) - 1, tiles_id, gcol))

    for ps in range(NPASS):
        for l in range(4):
            tid = "A" if (l < 2 and ps == 0) else "B"
            tiles = A_TILES if tid == "A" else B_TILES
            if l == 2:
                add_stage("kv", kv_group(ps, A_TILES if ps == 0 else B_TILES), ps, "A" if ps == 0 else "B", GKV)
            add_stage("f1", ffn_groups(l, 1, tiles), ps, tid, G1 + 8 * l)
            if l < 2:
                add_stage("mA", mixA_groups(l, ps, tiles), ps, tid, GM + 8 * l)
            else:
                add_stage("mB", mixB_groups(l - 2, l, ps), ps, tid, GM + 8 * l)
            add_stage("f2", ffn_groups(l, 2, tiles), ps, tid, G2 + 8 * l)

    per_pass = len(seq) // NPASS
    seq[0][1](0)
    hooked = False
    for i, (tag, load, compute, ps, first, last, tid, gcol) in enumerate(seq):
        s = i % 2
        if i % per_pass == 0:
            prologue(ps)
        if i + 1 < len(seq):
            nxt = seq[i + 1][1]
            P.on_flush.append(lambda nxt=nxt, s2=(i + 1) % 2: nxt(s2))
        if tag == "kv":
            P.drain()
            if ps == 0:
                P.fence()
        P.mark('%s_p%d_%d' % (tag, ps, i))
        P.skip_norm = hooked and first
        hooked = False
        P.hook = None
        if last and tag != "kv" and i + 1 < len(seq) and (i + 1) % per_pass != 0 and (stop_after is None or i + 1 < stop_after):
            ntag, _, _, _, nfirst, _, ntid, ngcol = seq[i + 1]
            if nfirst and ntid == tid:
                tl = A_TILES if tid == "A" else B_TILES
                P.hook = (lambda ti, tl=tl, ngcol=ngcol: norm_hook(ti, tl[ti][0], tl[ti][1], ngcol))
                hooked = True
        compute(s)
        P.hook = None
        if tag == "kv":
            P.drain()
            if ps == 0:
                P.fence()
        if (i + 1) % per_pass == 0:
            epilogue(ps)
        if stop_after is not None and i + 1 >= stop_after:
            P.flush()
            if (i + 1) % per_pass != 0:
                epilogue(ps)
            break
    P.drain()
    P.fence()
    for r in P.res.values():
        if r.dsem is not None and r.dcount > 0:
            SP.h.wait_ge(r.dsem, r.dcount)
    P.es.close()
    nc._marks = P.marks
    return nc


_HEAD_PAIR = []
for _m in range(2):
    for _i in range(4):
        _HEAD_PAIR.append((8 * _m + _i, 8 * _m + 4 + _i))


def _partner(d):
    return d + 8 if d < 8 else (d - 8 if d < 16 else d)


def _prep_shared(inp):
    f = lambda a: np.ascontiguousarray(a, dtype=np.float32)
    w_q, w_o, w_kv = inp["w_q"], inp["w_o"], inp["w_kv"]
    colperm = []
    colperm_s = []
    for (ha, hb) in _HEAD_PAIR:
        for hh in (ha, hb):
            colperm += [64 * hh + d for d in range(64)]
            colperm_s += [64 * hh + _partner(d) for d in range(64)]
    colperm = np.array(colperm); colperm_s = np.array(colperm_s)
    kperm_s = np.array([64 * g + _partner(d) for g in range(4) for d in range(64)])
    sh = {
        "w1g": f(inp["w_ffn1_gate"]), "w1u": f(inp["w_ffn1_up"]), "w1d": f(inp["w_ffn1_down"]),
        "w2g": f(inp["w_ffn2_gate"]), "w2u": f(inp["w_ffn2_up"]), "w2d": f(inp["w_ffn2_down"]),
        "win": f(inp["w_in_a"]), "wout": f(inp["w_out_a"]), "wkv": f(w_kv),
        "wks": f(w_kv[:, 0:256][:, kperm_s]),
        "wqp": f(w_q[:, :, colperm]), "wqs": f(w_q[:, :, colperm_s]), "wop": f(w_o[:, colperm, :]),
        "ident": np.eye(128, dtype=np.float32),
    }
    smalls = np.zeros((128, NSM), np.float32)
    for l in range(4):
        for k in range(8):
            smalls[:, 0 + l * 8 + k] = inp["g_ffn1"][l, k * 128:(k + 1) * 128]
            smalls[:, 32 + l * 8 + k] = inp["g_mix"][l, k * 128:(k + 1) * 128]
            smalls[:, 64 + l * 8 + k] = inp["g_ffn2"][l, k * 128:(k + 1) * 128]
    for k in range(8):
        smalls[:, 96 + k] = inp["g_kv"][k * 128:(k + 1) * 128]
    for l in range(2):
        for tap in range(3):
            for j in range(8):
                smalls[:, 104 + l * 24 + tap * 8 + j] = inp["conv_w"][l, tap, j * 128:(j + 1) * 128]
    d_idx = np.arange(128) % 64
    pd_idx = np.array([_partner(d) for d in d_idx])
    for j in range(2):
        smalls[:, 152 + j] = inp["g_qnorm"][j][d_idx]
        smalls[:, 154 + j] = inp["g_qnorm"][j][pd_idx]
    smalls[:, 156] = inp["g_knorm"][d_idx]
    smalls[:, 157] = inp["g_knorm"][pd_idx]
    inv_freq = (np.float32(500000.0) ** (-np.arange(0, 16, 2, dtype=np.float32) / np.float32(16))).astype(np.float64)
    fq = np.where(d_idx < 16, inv_freq[d_idx % 8] / (2 * math.pi), 0.0)
    smalls[:, 158] = fq.astype(np.float32)
    smalls[:, 159] = np.where(d_idx < 8, -1.0, np.where(d_idx < 16, 1.0, 0.0)).astype(np.float32)
    for j in range(2):
        for c, (ha, hb) in enumerate(_HEAD_PAIR):
            smalls[0:64, 160 + j * 8 + c] = inp["sinks"][j, ha]
            smalls[64:128, 160 + j * 8 + c] = inp["sinks"][j, hb]
    sh["smalls"] = smalls
    kk = np.arange(128)[:, None]
    qq = np.arange(128)[None, :]
    mprev = (kk > qq).astype(np.float32)
    mcur = (kk <= qq).astype(np.float32)
    sh["mask4"] = np.concatenate([mprev, mcur, mprev, mcur], axis=1)
    sh["_mask40_first"] = np.concatenate([np.zeros_like(mprev), mcur, np.zeros_like(mprev), mcur], axis=1)
    return sh


def _prep_core(inp, sh, core):
    xin = np.zeros((NPASS, T, D), np.float32)
    pos = np.zeros((NPASS, 128, T), np.float32)
    mask40 = np.zeros((NPASS, 128, 512), np.float32)
    stc = np.zeros((NPASS, 2, 2 * NS, D), np.float32)
    ck = np.zeros((NPASS, NS, 128, 256), np.float32)
    cv = np.zeros((NPASS, NS, 128, 256), np.float32)
    for ps in range(NPASS):
        v = core * NPASS + ps
        b, seg = v // 8, v % 8
        s0 = seg * OWN
        lo = s0 - HALO
        if lo >= 0:
            xin[ps, 0:HALO] = inp["x_prompt"][b, lo:s0]
            mask40[ps] = sh["mask4"]
        else:
            mask40[ps] = sh["_mask40_first"]
        xin[ps, HALO:HALO + OWN] = inp["x_prompt"][b, s0:s0 + OWN]
        xin[ps, HALO + OWN:] = inp["x_sample"][v * NS:(v + 1) * NS, 0]
        p = np.concatenate([np.maximum(np.arange(lo, s0 + OWN), 0), np.full(NS, 8192)]).astype(np.float32)
        pos[ps] = p[None, :]
        stc[ps] = inp["state_conv"][:, v * NS:(v + 1) * NS].reshape(2, 2 * NS, D)
        ck[ps] = inp["cache_k"][v * NS:(v + 1) * NS].reshape(NS, 128, 256)
        cv[ps] = inp["cache_v"][v * NS:(v + 1) * NS].reshape(NS, 128, 256)
    m = {k: v for k, v in sh.items() if not k.startswith("_")}
    m.update({"xin": xin, "pos": pos, "mask40": mask40, "stc": stc, "ck": ck, "cv": cv})
    return m


_NC_CACHE = {}


def kernel(**inputs):
    inp = {k: np.asarray(v) for k, v in inputs.items()}
    sh = _prep_shared(inp)
    in_maps = [_prep_core(inp, sh, c) for c in range(NCORES)]
    if "nc" not in _NC_CACHE:
        _NC_CACHE["nc"] = build_program()
    res = run_bass_kernel_spmd(_NC_CACHE["nc"], in_maps, core_ids=list(range(NCORES)))
    R = res.results
    B, S = 2, 8192
    y_prompt = np.zeros((B, S, D), np.float32)
    y_sample = np.zeros((128, 1, D), np.float32)
    conv_p = np.zeros((2, B, 2, D), np.float32)
    k_p = np.zeros((B, 128, 4, 64), np.float32)
    v_p = np.zeros((B, 128, 4, 64), np.float32)
    conv_s = np.zeros((2, 128, 2, D), np.float32)
    k_s = np.zeros((128, 128, 4, 64), np.float32)
    v_s = np.zeros((128, 128, 4, 64), np.float32)
    for core in range(NCORES):
        r = R[core]
        for ps in range(NPASS):
            v = core * NPASS + ps
            b, seg = v // 8, v % 8
            y_prompt[b, seg * OWN:(seg + 1) * OWN] = r["y_own"][ps]
            y_sample[v * NS:(v + 1) * NS, 0] = r["y_s"][ps]
            conv_s[:, v * NS:(v + 1) * NS] = r["conv_s_o"][ps]
            k_s[v * NS:(v + 1) * NS] = r["k_s_o"][ps].reshape(NS, 128, 4, 64)
            v_s[v * NS:(v + 1) * NS] = r["v_s_o"][ps].reshape(NS, 128, 4, 64)
            if seg == 7:
                conv_p[:, b] = r["conv_p_o"][ps]
                k_p[b] = r["k_p_o"][ps].reshape(128, 4, 64)
                v_p[b] = r["v_p_o"][ps].reshape(128, 4, 64)
    return (y_prompt, y_sample, conv_p, k_p, v_p, conv_s, k_s, v_s)
```

```python
import contextlib
import math

import numpy as np
import concourse.bass as bass
import concourse.mybir as mybir
from concourse.bass_utils import run_bass_kernel_spmd

F32 = mybir.dt.float32
BF16 = mybir.dt.bfloat16
AF = mybir.ActivationFunctionType
ALU = mybir.AluOpType

D = 1024
DFF = 2816
NCH = 8
HALO = 132
OWN = 1024
NS = 8
T = HALO + OWN + NS
NPASS = 2
NCORES = 8
EPS = 1e-6
A_TILES = [(0, 388), (388, 388), (776, 388)]
B_TILES = [(132, 344), (476, 344), (820, 344)]
TW = 388
FFN_GROUPS = [(0, 4), (4, 3), (7, 3), (10, 4), (14, 4), (18, 4)]
MAGIC = 12582912.0
TWO_PI_SAFE = 6.28318
NSM = 176
SAME_ENG_SYNC = True
import os
DBG = set(os.environ.get('KDBG', '').split(','))


class Res:
    __slots__ = ("name", "w", "r", "dsem", "dcount")

    def __init__(self, name):
        self.name = name
        self.w = None
        self.r = {}
        self.dsem = None
        self.dcount = 0


class Eng:
    def __init__(self, P, name, h, is_pe=False):
        self.P = P
        self.name = name
        self.h = h
        self.sem = P.es.enter_context(P.nc.semaphore("e_" + name))
        self.count = 0
        self.seen = {}
        self.is_pe = is_pe
        self.key = ("eng", name)


class Prog:
    def __init__(self):
        self.nc = bass.Bass("TRN2", target_bir_lowering=False)
        self.es = contextlib.ExitStack()
        self.res = {}
        self.pending = None
        self.on_flush = []
        nc = self.nc
        self.PE = Eng(self, "pe", nc.tensor, is_pe=True)
        self.ACT = Eng(self, "act", nc.scalar)
        self.DVE = Eng(self, "dve", nc.vector)
        self.POOL = Eng(self, "pool", nc.gpsimd)
        self.SP = Eng(self, "sp", nc.sync)
        self.engs = [self.PE, self.ACT, self.DVE, self.POOL, self.SP]
        self.out_res = []
        self.ucount = 0
        self.bcount = 0
        self.marks = []
        self.after_gu = []
        self.trickle = []
        self.kc_deferred = None
        self.npe = 0
        self.hook = None
        self.skip_norm = False

    def sb(self, name, shape, dt):
        return self.es.enter_context(self.nc.sbuf_tensor(name, list(shape), dt))

    def psum(self, name, shape, dt):
        return self.es.enter_context(self.nc.psum_tensor(name, list(shape), dt))

    def din(self, name, shape):
        return self.nc.dram_tensor(name, list(shape), F32, kind="ExternalInput").ap()

    def dout(self, name, shape):
        return self.nc.dram_tensor(name, list(shape), F32, kind="ExternalOutput").ap()

    def R(self, name):
        r = self.res.get(name)
        if r is None:
            r = Res(name)
            self.res[name] = r
        return r

    def _rl(self, xs):
        return [self.R(x) if isinstance(x, str) else x for x in xs]

    def _wait(self, eng, dep):
        sem, val, key = dep
        if key == eng.key and (eng.is_pe or not SAME_ENG_SYNC):
            return
        if eng.seen.get(key, 0) >= val:
            return
        eng.h.wait_ge(sem, val)
        eng.seen[key] = val

    def _deps(self, eng, reads, writes):
        for r in reads:
            if r.w is not None:
                self._wait(eng, r.w)
        for w in writes:
            if w.w is not None:
                self._wait(eng, w.w)
            for dep in list(w.r.values()):
                self._wait(eng, dep)

    def _record(self, stamp, reads, writes):
        key = stamp[2]
        for r in reads:
            r.r[key] = stamp
        for w in writes:
            w.w = stamp
            w.r = {}

    def op(self, eng, fn, reads=(), writes=()):
        reads = self._rl(reads)
        writes = self._rl(writes)
        bank_reads = [r for r in reads if r.name.startswith("bank")]
        if bank_reads:
            reads = [r for r in reads if not r.name.startswith("bank")]
            writes = writes + [r for r in bank_reads if r not in writes]
        self._deps(eng, reads, writes)
        if eng.is_pe:
            n0 = self._pecount()
        inst = fn()
        eng.count += 1
        inst.then_inc(eng.sem, 1)
        self._record((eng.sem, eng.count, eng.key), reads, writes)
        return inst

    def _pecount(self):
        return 0

    def mark(self, label):
        self.marks.append((self.PE.count, label))

    def dma(self, q, out, in_, reads=(), writes=(), track=None):
        return self.dma_batch(q, [(out, in_)], reads, writes, track)

    def dma_batch(self, q, pairs, reads=(), writes=(), track=None):
        reads = self._rl(reads)
        writes = self._rl(writes)
        track = self.R(track) if isinstance(track, str) else track
        self._deps(q, reads, writes)
        if track.dsem is None:
            track.dsem = self.es.enter_context(self.nc.semaphore("d_" + track.name))
        prev = (track.dsem, track.dcount, ("dma", track.name))
        if track.dcount > 0:
            self._wait(q, prev)
        for (out, in_) in pairs:
            inst = q.h.dma_start(out=out, in_=in_)
            track.dcount += 16
            inst.then_inc(track.dsem, 16)
        self._record((track.dsem, track.dcount, ("dma", track.name)), reads, writes)
        return inst

    def fence(self):
        for e in self.engs:
            for o in self.engs:
                if o is not e and o.count > 0:
                    self._wait(e, (o.sem, o.count, o.key))

    def run_after_gu(self):
        fl = self.after_gu
        self.after_gu = []
        for f in fl:
            f()

    def run_trickle(self, n=None):
        k = len(self.trickle) if n is None else min(n, len(self.trickle))
        fl = self.trickle[:k]
        self.trickle = self.trickle[k:]
        for f in fl:
            f()

    def flush(self):
        self.run_after_gu()
        self.run_trickle()
        if self.pending is not None:
            f = self.pending
            self.pending = None
            f()
        fl = self.on_flush
        self.on_flush = []
        for f in fl:
            f()

    def drain(self):
        self.flush()
        self.run_after_gu()
        self.run_trickle()

    def unit(self, gu_fn, d_fn):
        gu_fn()
        self.run_after_gu()
        self.run_trickle()
        self.flush()
        self.pending = d_fn


def build_program(stop_after=None):
    P = Prog()
    nc = P.nc
    PE, ACT, DVE, POOL, SP = P.PE, P.ACT, P.DVE, P.POOL, P.SP
    op, dma = P.op, P.dma

    xin = P.din("xin", [NPASS, T, D])
    posd = P.din("pos", [NPASS, 128, T])
    mask4d = P.din("mask4", [128, 512])
    mask40d = P.din("mask40", [NPASS, 128, 512])
    identd = P.din("ident", [128, 128])
    smallsd = P.din("smalls", [128, NSM])
    stcd = P.din("stc", [NPASS, 2, 2 * NS, D])
    ckd = P.din("ck", [NPASS, NS, 128, 256])
    cvd = P.din("cv", [NPASS, NS, 128, 256])
    w1g = P.din("w1g", [4, D, DFF]); w1u = P.din("w1u", [4, D, DFF]); w1d = P.din("w1d", [4, DFF, D])
    w2g = P.din("w2g", [4, D, DFF]); w2u = P.din("w2u", [4, D, DFF]); w2d = P.din("w2d", [4, DFF, D])
    wind = P.din("win", [2, D, 3 * D]); woutd = P.din("wout", [2, D, D])
    wkvd = P.din("wkv", [D, 512]); wksd = P.din("wks", [D, 256])
    wqpd = P.din("wqp", [2, D, D]); wqsd = P.din("wqs", [2, D, D]); wopd = P.din("wop", [2, D, D])

    y_own = P.dout("y_own", [NPASS, OWN, D])
    y_s = P.dout("y_s", [NPASS, NS, D])
    conv_s_o = P.dout("conv_s_o", [NPASS, 2, NS, 2, D])
    conv_p_o = P.dout("conv_p_o", [NPASS, 2, 2, D])
    k_p_o = P.dout("k_p_o", [NPASS, 128, 256])
    v_p_o = P.dout("v_p_o", [NPASS, 128, 256])
    k_s_o = P.dout("k_s_o", [NPASS, NS, 128, 256])
    v_s_o = P.dout("v_s_o", [NPASS, NS, 128, 256])

    x_sb = P.sb("x_sb", [128, NCH, T], F32)
    h_sb = P.sb("h_sb", [128, NCH, T], BF16)
    wA = [P.sb("wA%d" % i, [128, 8, 1024], BF16) for i in range(2)]
    wB = [P.sb("wB%d" % i, [128, 4, 1024], BF16) for i in range(2)]
    act = [P.sb("act%d" % i, [128, 4, TW], BF16) for i in range(2)]
    sg = [P.sb("sg%d" % i, [128, TW], BF16) for i in range(2)]
    sq = P.sb("sq", [128, NCH, TW], BF16)
    lnt = P.sb("lnt", [128, TW], F32)
    qsq = P.sb("qsq", [128, 2, TW], BF16)
    qrs = P.sb("qrs", [128, TW], F32)
    rstd = [P.sb("rstd%d" % i, [128, TW], F32) for i in range(2)]
    cosT = P.sb("cosT", [128, T], F32)
    sinT = P.sb("sinT", [128, T], F32)
    pos_sb = P.sb("pos_sb", [128, T], F32)
    kT = P.sb("kT", [128, 2, T], BF16)
    V_sb = P.sb("V_sb", [128, 9, 256], BF16)
    Kc = P.sb("Kc", [128, NS, 256], BF16)
    Vc = P.sb("Vc", [128, NS, 256], BF16)
    KTa = P.sb("KTa", [128, 2, NS, 128], BF16)
    qn = [P.sb("qn%d" % i, [128, OWN + NS], BF16) for i in range(2)]
    oT = [P.sb("oT%d" % i, [128, OWN + NS], BF16) for i in range(4)]
    t1 = P.sb("t1", [128, TW], F32)
    t2 = P.sb("t2", [128, TW], F32)
    PT = [P.sb("PT%d" % i, [128, 512], BF16) for i in range(4)]
    mask4 = P.sb("mask4s", [128, 512], BF16)
    mask40 = P.sb("mask40s", [128, 512], BF16)
    xs = [P.sb("xs%d" % i, [128, OWN + NS], F32) for i in range(2)]
    c_sb = P.sb("c_sb", [128, TW], F32)
    cu = [P.sb("cu%d" % i, [128, TW + 2], F32) for i in range(2)]
    conv = P.sb("conv", [128, TW], F32)
    zb = [P.sb("zb%d" % i, [128, 2, TW], BF16) for i in range(2)]
    st_fm = P.sb("st_fm", [128, 2, NCH, 2 * NS], F32)
    sv = P.sb("sv", [128, 2, NCH, NS + 2], F32)
    ktail = P.sb("ktail", [128, 2, 136], F32)
    kp_tok = P.sb("kp_tok", [128, 256], F32)
    vp_tok = P.sb("vp_tok", [128, 256], F32)
    ks_tok = P.sb("ks_tok", [NS, 256], F32)
    vs_tok = P.sb("vs_tok", [NS, 256], F32)
    ident = P.sb("ident_s", [128, 128], F32)
    identb = P.sb("identb", [128, 128], BF16)
    ones_mean = P.sb("ones_mean", [128, 128], BF16)
    blockones = P.sb("blockones", [128, 128], BF16)
    ones_bf = P.sb("ones_bf", [128, 128], BF16)
    smalls = P.sb("smalls_s", [128, NSM], F32)
    esink = P.sb("esink", [128, 16], F32)
    gsg = P.sb("gsg", [128, 4], F32)
    eps_t = P.sb("eps_t", [128, 1], F32)
    PTs = [P.sb("PTs%d" % i, [128, 2 * NS], BF16) for i in range(2)]

    banks = [P.psum("bank%d" % i, [128, 512], F32) for i in range(7)]
    bankb = P.psum("bankb", [128, 1024], BF16)

    G1, GM, G2, GKV, CW = 0, 32, 64, 96, 104
    GQ, GQS, GK, GKS, FQ, SGN, SINK = 152, 154, 156, 157, 158, 159, 160

    def sm(c):
        return smalls[:, c:c + 1]

    def XN(ti, ds=range(NCH)):
        return ["x_%d_%d" % (ti, d) for d in ds]

    def HN(ti):
        return ["h_%d_%d" % (ti, k) for k in range(NCH)]

    dma(SP, smalls[:], smallsd, writes=["smalls"], track="smalls")
    dma(SP, ident[:], identd, writes=["ident"], track="ident")
    dma(SP, xs[0][:, 0:512], mask4d, writes=["xs0"], track="xs0")
    op(DVE, lambda: nc.vector.tensor_copy(out=mask4[:], in_=xs[0][:, 0:512]), reads=["xs0"], writes=["mask4"])
    op(DVE, lambda: nc.vector.tensor_copy(out=identb[:], in_=ident[:]), reads=["ident"], writes=["identb"])
    op(DVE, lambda: nc.vector.memset(ones_mean[:], 1.0 / D), writes=["ones_mean"])
    op(DVE, lambda: nc.vector.memset(ones_bf[:], 1.0), writes=["ones_bf"])
    op(DVE, lambda: nc.vector.memset(blockones[:], 0.0), writes=["blockones"])
    op(DVE, lambda: nc.vector.memset(blockones[0:64, 0:64], 1.0 / 64), writes=["blockones"])
    op(DVE, lambda: nc.vector.memset(blockones[64:128, 64:128], 1.0 / 64), writes=["blockones"])
    op(DVE, lambda: nc.vector.memset(eps_t[:], EPS), writes=["eps"])
    op(DVE, lambda: nc.vector.memset(cu[0][:], 0.0), writes=["cu0"])
    op(DVE, lambda: nc.vector.memset(cu[1][:], 0.0), writes=["cu1"])
    op(DVE, lambda: nc.vector.tensor_scalar(out=gsg[:, 0:2], in0=smalls[:, GQS:GQS + 2], scalar1=sm(SGN), scalar2=None, op0=ALU.mult),
       reads=["smalls"], writes=["gsg"])
    op(DVE, lambda: nc.vector.tensor_scalar(out=gsg[:, 2:3], in0=smalls[:, GKS:GKS + 1], scalar1=sm(SGN), scalar2=None, op0=ALU.mult),
       reads=["smalls"], writes=["gsg"])
    op(ACT, lambda: nc.scalar.activation(out=esink[:], in_=smalls[:, SINK:SINK + 16], func=AF.Exp), reads=["smalls"], writes=["esink"])

    groups = []

    def slotres(s):
        return P.R("wslot%d" % s)

    def wload(s, *pairs):
        P.dma_batch(POOL, list(pairs), writes=[slotres(s)], track=slotres(s))

    def norm_tile_a(ti, c0, n):
        op(ACT, lambda: nc.scalar.activation(out=sq[:, :, 0:n], in_=x_sb[:, :, c0:c0 + n], func=AF.Square),
           reads=XN(ti), writes=["sq"])

    def norm_tile_b1(ti, c0, n):
        bk = banks[6]

        def mm():
            for k in range(NCH):
                i = nc.tensor.matmul(bk[:, 0:n], lhsT=ones_mean[:], rhs=sq[:, k, 0:n], start=(k == 0), stop=(k == NCH - 1))
            return i
        op(PE, mm, reads=["sq", "ones_mean"], writes=["bank6"])
        rs = rstd[ti % 2]
        rsn = "rstd%d" % (ti % 2)
        op(ACT, lambda: nc.scalar.activation(out=lnt[:, 0:n], in_=bk[:, 0:n], func=AF.Ln, bias=eps_t[:, 0:1]),
           reads=["bank6", "eps"], writes=["lnt"])
        op(ACT, lambda: nc.scalar.activation(out=rs[:, 0:n], in_=lnt[:, 0:n], func=AF.Exp, scale=-0.5),
           reads=["lnt"], writes=[rsn])

    def norm_tile_b2(ti, c0, n, gcol):
        xr = "x_%d" % ti
        rs = rstd[ti % 2]
        rsn = "rstd%d" % (ti % 2)
        return [(lambda k=k: op(DVE, lambda: nc.vector.scalar_tensor_tensor(out=h_sb[:, k, c0:c0 + n], in0=x_sb[:, k, c0:c0 + n], scalar=sm(gcol + k),
                                                                            in1=rs[:, 0:n], op0=ALU.mult, op1=ALU.mult),
                                reads=["x_%d_%d" % (ti, k), rsn, "smalls"], writes=["h_%d_%d" % (ti, k)])) for k in range(NCH)]

    def norm_tile_b(ti, c0, n, gcol):
        norm_tile_b1(ti, c0, n)
        for f in norm_tile_b2(ti, c0, n, gcol):
            f()

    def norm_tile(ti, c0, n, gcol):
        norm_tile_a(ti, c0, n)
        norm_tile_b(ti, c0, n, gcol)

    def norm_hook(ti, c0, n, gcol):
        norm_tile_a(ti, c0, n)

        def b1():
            norm_tile_b1(ti, c0, n)
            P.trickle.extend(norm_tile_b2(ti, c0, n, gcol))
        P.after_gu.append(b1)

    def norm_stage(tiles, gcol):
        if P.skip_norm:
            P.skip_norm = False
            return
        P.drain()
        for ti, (c0, n) in enumerate(tiles):
            norm_tile(ti, c0, n, gcol)

    def ffn_groups(l, which, tiles):
        wg, wu, wd = (w1g, w1u, w1d) if which == 1 else (w2g, w2u, w2d)
        gcol = (G1 if which == 1 else G2) + 8 * l
        out = []
        for gi, (fc0, G) in enumerate(FFN_GROUPS):
            def load(s, fc0=fc0, G=G):
                wload(s, (wA[s][:, :, 0:G * 128], wg[l].rearrange("(k p) n -> p k n", p=128)[:, :, fc0 * 128:(fc0 + G) * 128]),
                      (wA[s][:, :, 512:512 + G * 128], wu[l].rearrange("(k p) n -> p k n", p=128)[:, :, fc0 * 128:(fc0 + G) * 128]),
                      (wB[s][:, 0:G, :], wd[l][fc0 * 128:(fc0 + G) * 128, :].rearrange("(j p) n -> p j n", p=128)))

            def compute(s, gi=gi, G=G):
                hook = P.hook
                if gi == 0:
                    norm_stage(tiles, gcol)
                for ti, (c0, n) in enumerate(tiles):
                    P.ucount += 1
                    ab = P.ucount % 2

                    def gu(ti=ti, c0=c0, n=n, ab=ab):
                        for j in range(G):
                            pg = banks[j % 2]; pu = banks[2 + j % 2]
                            pgn = "bank%d" % (j % 2); pun = "bank%d" % (2 + j % 2)

                            def mmg(j=j, pg=pg):
                                for k in range(NCH):
                                    i = nc.tensor.matmul(pg[:, 0:n], lhsT=wA[s][:, k, j * 128:(j + 1) * 128], rhs=h_sb[:, k, c0:c0 + n],
                                                         start=(k == 0), stop=(k == NCH - 1))
                                return i

                            def mmu(j=j, pu=pu):
                                for k in range(NCH):
                                    i = nc.tensor.matmul(pu[:, 0:n], lhsT=wA[s][:, k, 512 + j * 128:512 + (j + 1) * 128], rhs=h_sb[:, k, c0:c0 + n],
                                                         start=(k == 0), stop=(k == NCH - 1))
                                return i
                            op(PE, mmg, reads=[slotres(s)] + HN(ti), writes=[pgn])
                            op(PE, mmu, reads=[slotres(s)] + HN(ti), writes=[pun])
                            sgj = sg[j % 2]; sgn_ = "sg%d" % (j % 2)
                            op(ACT, lambda pg=pg, sgj=sgj: nc.scalar.activation(out=sgj[:, 0:n], in_=pg[:, 0:n], func=AF.Silu),
                               reads=[pgn], writes=[sgn_])
                            op(DVE, lambda j=j, pu=pu, sgj=sgj: nc.vector.tensor_tensor(out=act[ab][:, j, 0:n], in0=pu[:, 0:n], in1=sgj[:, 0:n], op=ALU.mult),
                               reads=[pun, sgn_], writes=["act%d_%d" % (ab, j)])
                            if j == 0:
                                P.run_after_gu()
                            else:
                                P.run_trickle(3)

                    def dn(ti=ti, c0=c0, n=n, ab=ab):
                        for d in range(NCH):
                            pd = banks[4 + d % 3]; pdn = "bank%d" % (4 + d % 3)

                            def mmd(d=d, pd=pd):
                                for j in range(G):
                                    i = nc.tensor.matmul(pd[:, 0:n], lhsT=wB[s][:, j, d * 128:(d + 1) * 128], rhs=act[ab][:, j, 0:n],
                                                         start=(j == 0), stop=(j == G - 1))
                                return i
                            op(PE, mmd, reads=[slotres(s)] + ["act%d_%d" % (ab, j) for j in range(G)], writes=[pdn])
                            op(DVE, lambda d=d, pd=pd: nc.vector.scalar_tensor_tensor(out=x_sb[:, d, c0:c0 + n], in0=pd[:, 0:n], scalar=0.5,
                                                                                     in1=x_sb[:, d, c0:c0 + n], op0=ALU.mult, op1=ALU.add),
                               reads=[pdn], writes=["x_%d_%d" % (ti, d)])
                        if hook is not None:
                            hook(ti)
                    P.unit(gu, dn)
            out.append((load, compute))
        return out

    def mixA_groups(l, ps, MT):
        out = []
        for gi in range(4):
            def load(s, gi=gi):
                prs = [(wA[s][:, :, part * 256:(part + 1) * 256],
                        wind[l].rearrange("(k p) n -> p k n", p=128)[:, :, part * D + gi * 256: part * D + (gi + 1) * 256]) for part in range(3)]
                prs.append((wB[s][:, 0:2, :], woutd[l][gi * 256:(gi + 1) * 256, :].rearrange("(j p) n -> p j n", p=128)))
                wload(s, *prs)

            def compute(s, gi=gi):
                hook = P.hook
                if gi == 0:
                    norm_stage(MT, GM + 8 * l)
                cui = [0]
                for ti, (c0, n) in enumerate(MT):
                    P.ucount += 1
                    zi = P.ucount % 2

                    def gu(ti=ti, c0=c0, n=n, zi=zi):
                        for jj in range(2):
                            j = gi * 2 + jj
                            pb, pc, pu = banks[jj], banks[2], banks[3]
                            pbn, pcn, pun = "bank%d" % jj, "bank2", "bank3"

                            def mm(part, bk, jj=jj):
                                def f():
                                    for k in range(NCH):
                                        i = nc.tensor.matmul(bk[:, 0:n], lhsT=wA[s][:, k, part * 256 + jj * 128: part * 256 + (jj + 1) * 128],
                                                             rhs=h_sb[:, k, c0:c0 + n], start=(k == 0), stop=(k == NCH - 1))
                                    return i
                                return f
                            op(PE, mm(1, pc), reads=[slotres(s)] + HN(ti), writes=[pcn])
                            op(PE, mm(2, pu), reads=[slotres(s)] + HN(ti), writes=[pun])
                            op(PE, mm(0, pb), reads=[slotres(s)] + HN(ti), writes=[pbn])
                            cb = cu[jj]; cbn = "cu%d" % jj
                            op(ACT, lambda pc=pc: nc.scalar.copy(out=c_sb[:, 0:n], in_=pc[:, 0:n]), reads=[pcn], writes=["c_sb"])
                            if ti == 0 and ps == 0:
                                op(DVE, lambda cb=cb: nc.vector.memset(cb[:, 0:2], 0.0), writes=[cbn])
                            elif ti == 0:
                                op(DVE, lambda cb=cb, j=j: nc.vector.tensor_copy(out=cb[:, 0:2], in_=sv[:, l, j, NS:NS + 2]), reads=["sv"], writes=[cbn])
                            else:
                                pn = MT[ti - 1][1]
                                op(DVE, lambda cb=cb, pn=pn: nc.vector.tensor_copy(out=cb[:, 0:2], in_=cb[:, pn:pn + 2]), reads=[cbn], writes=[cbn])
                            op(DVE, lambda cb=cb, pu=pu: nc.vector.tensor_tensor(out=cb[:, 2:2 + n], in0=pu[:, 0:n], in1=c_sb[:, 0:n], op=ALU.mult),
                               reads=[pun, "c_sb"], writes=[cbn])
                            cw0, cw1, cw2 = (sm(CW + l * 24 + tap * 8 + j) for tap in range(3))
                            op(ACT, lambda cb=cb, cw0=cw0: nc.scalar.mul(out=conv[:, 0:n], in_=cb[:, 0:n], mul=cw0),
                               reads=[cbn, "smalls"], writes=["conv"])
                            op(DVE, lambda cb=cb, cw1=cw1: nc.vector.scalar_tensor_tensor(out=conv[:, 0:n], in0=cb[:, 1:1 + n], scalar=cw1, in1=conv[:, 0:n],
                                                                                         op0=ALU.mult, op1=ALU.add), reads=[cbn, "conv", "smalls"], writes=["conv"])
                            op(DVE, lambda cb=cb, cw2=cw2: nc.vector.scalar_tensor_tensor(out=conv[:, 0:n], in0=cb[:, 2:2 + n], scalar=cw2, in1=conv[:, 0:n],
                                                                                         op0=ALU.mult, op1=ALU.add), reads=[cbn, "conv", "smalls"], writes=["conv"])
                            if ti == len(MT) - 1:
                                so = n - NS
                                stv = st_fm[:, l, j, :].rearrange("p (s r) -> p s r", r=2)
                                op(DVE, lambda cb=cb, cw2=cw2, so=so: nc.vector.tensor_scalar(out=conv[:, so:n], in0=cb[:, 2 + so:2 + n], scalar1=cw2, scalar2=None, op0=ALU.mult),
                                   reads=[cbn, "smalls"], writes=["conv"])
                                op(DVE, lambda stv=stv, cw1=cw1, so=so: nc.vector.scalar_tensor_tensor(out=conv[:, so:n], in0=stv[:, :, 1], scalar=cw1, in1=conv[:, so:n],
                                                                                                      op0=ALU.mult, op1=ALU.add), reads=["st_fm", "conv", "smalls"], writes=["conv"])
                                op(DVE, lambda stv=stv, cw0=cw0, so=so: nc.vector.scalar_tensor_tensor(out=conv[:, so:n], in0=stv[:, :, 0], scalar=cw0, in1=conv[:, so:n],
                                                                                                      op0=ALU.mult, op1=ALU.add), reads=["st_fm", "conv", "smalls"], writes=["conv"])
                                op(ACT, lambda cb=cb, so=so: nc.scalar.copy(out=sv[:, l, j, 0:NS], in_=cb[:, 2 + so:2 + n]), reads=[cbn], writes=["sv"])
                                op(ACT, lambda cb=cb, so=so: nc.scalar.copy(out=sv[:, l, j, NS:NS + 2], in_=cb[:, so:so + 2]), reads=[cbn], writes=["sv"])
                            op(DVE, lambda pb=pb, jj=jj: nc.vector.tensor_tensor(out=zb[zi][:, jj, 0:n], in0=pb[:, 0:n], in1=conv[:, 0:n], op=ALU.mult),
                               reads=[pbn, "conv"], writes=["zb%d_%d" % (zi, jj)])
                            if jj == 0:
                                P.run_after_gu()
                            else:
                                P.run_trickle(4)

                    def dn(ti=ti, c0=c0, n=n, zi=zi):
                        for d in range(NCH):
                            pd = banks[4 + d % 3]; pdn = "bank%d" % (4 + d % 3)

                            def mmd(d=d, pd=pd):
                                for jj in range(2):
                                    i = nc.tensor.matmul(pd[:, 0:n], lhsT=wB[s][:, jj, d * 128:(d + 1) * 128], rhs=zb[zi][:, jj, 0:n],
                                                         start=(jj == 0), stop=(jj == 1))
                                return i
                            op(PE, mmd, reads=[slotres(s), "zb%d_0" % zi, "zb%d_1" % zi], writes=[pdn])
                            op(DVE, lambda d=d, pd=pd: nc.vector.tensor_tensor(out=x_sb[:, d, c0:c0 + n], in0=pd[:, 0:n], in1=x_sb[:, d, c0:c0 + n], op=ALU.add),
                               reads=[pdn], writes=["x_%d_%d" % (ti, d)])
                        if hook is not None:
                            hook(ti)
                    P.unit(gu, dn)
            out.append((load, compute))
        return out

    def qk_post_a(pq, pqn, n, sqi):
        op(ACT, lambda: nc.scalar.activation(out=qsq[:, sqi, 0:n], in_=pq[:, 0:n], func=AF.Square), reads=[pqn], writes=["qsq_%d" % sqi])

    def qk_post_b(pq, pqn, pq2, pq2n, c0, n, gcol_ap, gs_ap, outs, sqi):
        bk = banks[6]
        op(PE, lambda: nc.tensor.matmul(bk[:, 0:n], lhsT=blockones[:], rhs=qsq[:, sqi, 0:n], start=True, stop=True),
           reads=["qsq_%d" % sqi, "blockones"], writes=["bank6"])
        op(ACT, lambda: nc.scalar.activation(out=lnt[:, 0:n], in_=bk[:, 0:n], func=AF.Ln, bias=eps_t[:, 0:1]), reads=["bank6", "eps"], writes=["lnt"])
        op(ACT, lambda: nc.scalar.activation(out=qrs[:, 0:n], in_=lnt[:, 0:n], func=AF.Exp, scale=-0.5), reads=["lnt"], writes=["qrs"])
        op(DVE, lambda: nc.vector.scalar_tensor_tensor(out=t1[:, 0:n], in0=pq[:, 0:n], scalar=gcol_ap, in1=cosT[:, c0:c0 + n], op0=ALU.mult, op1=ALU.mult),
           reads=[pqn, "cosT", "smalls"], writes=["t1"])
        op(DVE, lambda: nc.vector.scalar_tensor_tensor(out=t2[:, 0:n], in0=pq2[:, 0:n], scalar=gs_ap, in1=sinT[:, c0:c0 + n], op0=ALU.mult, op1=ALU.mult),
           reads=[pq2n, "sinT", "gsg"], writes=["t2"])
        op(DVE, lambda: nc.vector.tensor_tensor(out=t1[:, 0:n], in0=t1[:, 0:n], in1=t2[:, 0:n], op=ALU.add), reads=["t1", "t2"], writes=["t1"])
        for (oap, oname, a, b) in outs:
            op(DVE, lambda oap=oap, a=a, b=b: nc.vector.tensor_tensor(out=oap, in0=t1[:, a:b], in1=qrs[:, a:b], op=ALU.mult),
               reads=["t1", "qrs"], writes=[oname])

    def qk_post(pq, pqn, pq2, pq2n, c0, n, gcol_ap, gs_ap, outs):
        qk_post_a(pq, pqn, n, 0)
        qk_post_b(pq, pqn, pq2, pq2n, c0, n, gcol_ap, gs_ap, outs, 0)

    def kv_group(ps, KT_):
        def load(s):
            wload(s, (wA[s][:, :, 0:256], wkvd.rearrange("(k p) n -> p k n", p=128)[:, :, 0:256]),
                  (wA[s][:, :, 256:512], wksd.rearrange("(k p) n -> p k n", p=128)),
                  (wA[s][:, :, 512:768], wkvd.rearrange("(k p) n -> p k n", p=128)[:, :, 256:512]))

        def compute(s):
            norm_stage(KT_, GKV)
            if ps == 1:
                op(DVE, lambda: nc.vector.tensor_copy(out=kT[:, :, 4:132], in_=kT[:, :, HALO + OWN - 128:HALO + OWN]), reads=["kT"], writes=["kT"])
                op(DVE, lambda: nc.vector.tensor_copy(out=V_sb[:, 0, :], in_=V_sb[:, 8, :]), reads=["V_sb"], writes=["V_sb"])
            if 'kv1' in DBG:
                return
            for ti, (c0, n) in enumerate(KT_):
                for m in range(2):
                    pk, pk2 = banks[m], banks[2 + m]
                    pkn, pk2n = "bank%d" % m, "bank%d" % (2 + m)

                    def mm(off, bk, m=m):
                        def f():
                            for k in range(NCH):
                                i = nc.tensor.matmul(bk[:, 0:n], lhsT=wA[s][:, k, off + m * 128: off + (m + 1) * 128], rhs=h_sb[:, k, c0:c0 + n],
                                                     start=(k == 0), stop=(k == NCH - 1))
                            return i
                        return f
                    op(PE, mm(0, pk), reads=[slotres(s)] + HN(ti), writes=[pkn])
                    op(PE, mm(256, pk2), reads=[slotres(s)] + HN(ti), writes=[pk2n])
                    outs = [(kT[:, m, c0:c0 + n], "kT", 0, n)]
                    if ti == 2:
                        a = (HALO + OWN - 128) - c0
                        outs.append((ktail[:, m, 0:136], "ktail", a, a + 136))
                    qk_post(pk, pkn, pk2, pk2n, c0, n, sm(GK), gsg[:, 2:3], outs)
            if 'kv2' in DBG:
                return
            for b in range(0 if ps == 0 else 1, 9):
                kc0 = 4 + 128 * b
                pv = banks[4 + b % 2]; pvn = "bank%d" % (4 + b % 2)
                hres = [nm for ti, (c0, n) in enumerate(KT_) if c0 < kc0 + 128 and kc0 < c0 + n for nm in HN(ti)]

                def mmv(kc0=kc0, pv=pv):
                    for k in range(NCH):
                        i = nc.tensor.matmul(pv[:, 0:256], lhsT=h_sb[:, k, kc0:kc0 + 128], rhs=wA[s][:, k, 512:768], start=(k == 0), stop=(k == NCH - 1))
                    return i
                op(PE, mmv, reads=[slotres(s)] + hres, writes=[pvn])
                op(ACT, lambda b=b, pv=pv: nc.scalar.copy(out=V_sb[:, b, :], in_=pv[:, 0:256]), reads=[pvn], writes=["V_sb"])
                if b == 8:
                    op(DVE, lambda pv=pv: nc.vector.tensor_copy(out=vp_tok[:], in_=pv[:, 0:256]), reads=[pvn], writes=["vp_tok"])
                    dma(SP, v_p_o[ps], vp_tok[:], reads=["vp_tok"], track="vp_tok")
            sc0 = HALO + OWN
            if 'no_vs' in DBG:
                return
            pv = banks[4]

            def mmvs():
                for k in range(NCH):
                    i = nc.tensor.matmul(pv[0:NS, 0:256], lhsT=h_sb[:, k, sc0:sc0 + NS], rhs=wA[s][:, k, 512:768], start=(k == 0), stop=(k == NCH - 1))
                return i
            op(PE, mmvs, reads=[slotres(s)] + HN(2), writes=["bank4"])
            op(DVE, lambda: nc.vector.tensor_copy(out=vs_tok[:], in_=pv[0:NS, 0:256]), reads=["bank4"], writes=["vs_tok"])
            dma(SP, v_s_o[ps, :, 127, :], vs_tok[:], reads=["vs_tok"], writes=["vso"], track="vso")
            pt = banks[5]
            if 'no_ktail' in DBG:
                return

            def trk():
                for m in range(2):
                    i = nc.tensor.transpose(out=pt[:, m * 128:(m + 1) * 128], in_=ktail[:, m, 0:128], identity=ident[:])
                return i
            op(PE, trk, reads=["ktail", "ident"], writes=["bank5"])
            op(DVE, lambda: nc.vector.tensor_copy(out=kp_tok[:], in_=pt[:, 0:256]), reads=["bank5"], writes=["kp_tok"])
            dma(SP, k_p_o[ps], kp_tok[:], reads=["kp_tok"], track="kp_tok")

            def trks():
                for m in range(2):
                    i = nc.tensor.transpose(out=pt[0:NS, 256 + m * 128:256 + (m + 1) * 128], in_=ktail[:, m, 128:136], identity=ident[:])
                return i
            op(PE, trks, reads=["ktail", "ident"], writes=["bank5"])
            op(DVE, lambda: nc.vector.tensor_copy(out=ks_tok[:], in_=pt[0:NS, 256:512]), reads=["bank5"], writes=["ks_tok"])
            dma(SP, k_s_o[ps, :, 127, :], ks_tok[:], reads=["ks_tok"], writes=["kso"], track="kso")
            if 'no_reload' in DBG:
                return
            dma(POOL, Kc[:], k_s_o[ps].rearrange("s k d -> k s d"), reads=["kso"], writes=["Kc"], track="Kc")
            dma(POOL, Vc[:], v_s_o[ps].rearrange("s k d -> k s d"), reads=["vso"], writes=["Vc"], track="Vc")
            def kc_transposes():
                for si in range(NS):
                    def trc(si=si):
                        for m in range(2):
                            i = nc.tensor.transpose(out=bankb[:, m * 128:(m + 1) * 128], in_=Kc[:, si, m * 128:(m + 1) * 128], identity=identb[:])
                        return i
                    op(PE, trc, reads=["Kc", "identb"], writes=["bankb"])
                    op(ACT, lambda si=si: nc.scalar.copy(out=KTa[:, :, si, :], in_=bankb[:, 0:256].rearrange("p (m k) -> p m k", m=2)),
                       reads=["bankb"], writes=["KTa"])
            P.kc_deferred = kc_transposes
        return [(load, compute)]

    def mixB_groups(jl, l, ps):
        out = []
        for gi in range(4):
            def load(s, gi=gi):
                wload(s, (wA[s][:, :, 0:256], wqpd[jl].rearrange("(k p) n -> p k n", p=128)[:, :, gi * 256:(gi + 1) * 256]),
                      (wA[s][:, :, 256:512], wqsd[jl].rearrange("(k p) n -> p k n", p=128)[:, :, gi * 256:(gi + 1) * 256]),
                      (wB[s][:, 0:2, :], wopd[jl][gi * 256:(gi + 1) * 256, :].rearrange("(j p) n -> p j n", p=128)))

            def compute(s, gi=gi):
                hook = P.hook
                if gi == 0:
                    norm_stage(B_TILES, GM + 8 * l)
                    P.flush()
                par = P.bcount % 2
                P.bcount += 1
                obs = [oT[par * 2 + cc] for cc in range(2)]
                obns = ["oT%d" % (par * 2 + cc) for cc in range(2)]

                def qproj2(sq_):
                    prev = None
                    step = 0
                    for ti, (c0, n) in enumerate(B_TILES):
                        for cc in range(2):
                            qb = qn[cc]; qbn = "qn%d" % cc
                            if step == 1:
                                P.run_after_gu()
                            elif step > 1:
                                P.run_trickle(4)
                            bi = (step % 2) * 2
                            pq, pq2 = banks[bi], banks[bi + 1]
                            pqn, pq2n = "bank%d" % bi, "bank%d" % (bi + 1)

                            def mm(off, bk, c0=c0, n=n, cc=cc):
                                def f():
                                    for k in range(NCH):
                                        i = nc.tensor.matmul(bk[:, 0:n], lhsT=wA[sq_][:, k, off + cc * 128: off + (cc + 1) * 128], rhs=h_sb[:, k, c0:c0 + n],
                                                             start=(k == 0), stop=(k == NCH - 1))
                                    return i
                                return f
                            op(PE, mm(0, pq), reads=[slotres(sq_)] + HN(ti), writes=[pqn])
                            op(PE, mm(256, pq2), reads=[slotres(sq_)] + HN(ti), writes=[pq2n])
                            qk_post_a(pq, pqn, n, step % 2)
                            if prev is not None:
                                prev()
                            prev = (lambda pq=pq, pqn=pqn, pq2=pq2, pq2n=pq2n, c0=c0, n=n, qb=qb, qbn=qbn, sqi=step % 2:
                                    qk_post_b(pq, pqn, pq2, pq2n, c0, n, sm(GQ + jl), gsg[:, jl:jl + 1], [(qb[:, c0 - HALO:c0 - HALO + n], qbn, 0, n)], sqi))
                            step += 1
                    prev()

                def attention2():
                    cs = [gi * 2 + cc for cc in range(2)]
                    ms = [c // 4 for c in cs]
                    qbs = [qn[cc] for cc in range(2)]; qbns = ["qn%d" % cc for cc in range(2)]
                    dalls = [xs[cc] for cc in range(2)]; dallns = ["xs%d" % cc for cc in range(2)]
                    ess = [esink[:, jl * 8 + c: jl * 8 + c + 1] for c in cs]

                    def st(cc, j):
                        q0 = 128 * j
                        m = ms[cc]; qb = qbs[cc]; qbn = qbns[cc]
                        pt = PT[cc * 2 + j % 2]; ptn = "PT%d" % (cc * 2 + j % 2)
                        for hd in range(2):
                            bk = banks[2 * cc + hd]; bkn = "bank%d" % (2 * cc + hd)

                            def f(hd=hd, bk=bk):
                                for kb in range(2):
                                    kc = 4 + 128 * (j + kb)
                                    i = nc.tensor.matmul(bk[:, kb * 128:(kb + 1) * 128], lhsT=kT[hd * 64:(hd + 1) * 64, m, kc:kc + 128],
                                                         rhs=qb[hd * 64:(hd + 1) * 64, q0:q0 + 128], start=True, stop=True)
                                return i
                            op(PE, f, reads=["kT", qbn], writes=[bkn])
                            op(ACT, lambda hd=hd, bk=bk: nc.scalar.activation(out=pt[:, hd * 256:(hd + 1) * 256], in_=bk[:, 0:256], func=AF.Exp, scale=0.125),
                               reads=[bkn], writes=[ptn + "_h%d" % hd])
                        mk = mask40 if j == 0 else mask4
                        op(DVE, lambda: nc.vector.tensor_tensor(out=pt[:], in0=pt[:], in1=mk[:], op=ALU.mult), reads=["mask4", "mask40"], writes=[ptn + "_h0", ptn + "_h1"])

                    def pv(cc, j):
                        m = ms[cc]; ob = obs[cc]; obn = obns[cc]; dall = dalls[cc]; dalln = dallns[cc]
                        bk = banks[4 + cc]; bkn = "bank%d" % (4 + cc)
                        pt = PT[cc * 2 + j % 2]; ptn = "PT%d" % (cc * 2 + j % 2)
                        q0 = 128 * j

                        def f():
                            for hd in range(2):
                                for kb in range(2):
                                    i = nc.tensor.matmul(bk[hd * 64:(hd + 1) * 64, 0:128], lhsT=V_sb[:, j + kb, m * 128 + hd * 64: m * 128 + (hd + 1) * 64],
                                                         rhs=pt[:, (hd * 2 + kb) * 128:(hd * 2 + kb + 1) * 128], start=(kb == 0), stop=(kb == 1))
                            for hd in range(2):
                                for kb in range(2):
                                    i = nc.tensor.matmul(bk[hd * 64:(hd + 1) * 64, 128:256], lhsT=ones_bf[:, 0:64],
                                                         rhs=pt[:, (hd * 2 + kb) * 128:(hd * 2 + kb + 1) * 128], start=(kb == 0), stop=(kb == 1))
                            return i
                        op(PE, f, reads=["V_sb", ptn + "_h0", ptn + "_h1", "ones_bf"], writes=[bkn])
                        op(ACT, lambda: nc.scalar.copy(out=ob[:, q0:q0 + 128], in_=bk[:, 0:128]), reads=[bkn], writes=[obn + "_b%d" % j])
                        op(DVE, lambda: nc.vector.tensor_scalar(out=dall[:, q0:q0 + 128], in0=bk[:, 128:256], scalar1=ess[cc][:, 0:1], scalar2=None, op0=ALU.add),
                           reads=[bkn, "esink"], writes=[dalln + "_b%d" % j])

                    def sample(cc):
                        m = ms[cc]; qb = qbs[cc]; qbn = qbns[cc]; ob = obs[cc]; obn = obns[cc]; dall = dalls[cc]; dalln = dallns[cc]
                        pts = PTs[cc]; ptsn = "PTs%d" % cc
                        for hd in range(2):
                            bk = banks[2 * cc + hd]; bkn = "bank%d" % (2 * cc + hd)

                            def fs(hd=hd, bk=bk):
                                for si in range(NS):
                                    i = nc.tensor.matmul(bk[:, si:si + 1], lhsT=KTa[hd * 64:(hd + 1) * 64, m, si, :],
                                                         rhs=qb[hd * 64:(hd + 1) * 64, OWN + si:OWN + si + 1], start=True, stop=True)
                                return i
                            op(PE, fs, reads=["KTa", qbn], writes=[bkn])
                            op(ACT, lambda hd=hd, bk=bk: nc.scalar.activation(out=pts[:, hd * NS:(hd + 1) * NS], in_=bk[:, 0:NS], func=AF.Exp, scale=0.125),
                               reads=[bkn], writes=[ptsn])
                        bo = banks[4 + cc]; bon = "bank%d" % (4 + cc)

                        def fo():
                            for si in range(NS):
                                for hd in range(2):
                                    i = nc.tensor.matmul(bo[hd * 64:(hd + 1) * 64, si:si + 1], lhsT=Vc[:, si, m * 128 + hd * 64: m * 128 + (hd + 1) * 64],
                                                         rhs=pts[:, hd * NS + si:hd * NS + si + 1], start=True, stop=True)
                            i = nc.tensor.matmul(bo[:, 128:128 + 2 * NS], lhsT=ones_bf[:], rhs=pts[:], start=True, stop=True)
                            return i
                        op(PE, fo, reads=["Vc", ptsn, "ones_bf"], writes=[bon])
                        op(DVE, lambda: nc.vector.tensor_copy(out=ob[:, OWN:OWN + NS], in_=bo[:, 0:NS]), reads=[bon], writes=[obn + "_b8"])
                        for hd in range(2):
                            pr = slice(hd * 64, (hd + 1) * 64)
                            op(DVE, lambda pr=pr, hd=hd: nc.vector.tensor_scalar(out=dall[pr, OWN:OWN + NS], in0=bo[pr, 128 + hd * NS:128 + (hd + 1) * NS],
                                                                               scalar1=ess[cc][pr, 0:1], scalar2=None, op0=ALU.add),
                               reads=[bon, "esink"], writes=[dalln + "_b8"])

                    def finalize(cc):
                        ob = obs[cc]; obn = obns[cc]; dall = dalls[cc]; dalln = dallns[cc]; es_ap = ess[cc]
                        W = OWN + NS
                        dbl = [dalln + "_b%d" % b for b in range(9)]
                        obl = [obn + "_b%d" % b for b in range(9)]
                        op(ACT, lambda: nc.scalar.activation(out=dall[:, 0:W], in_=dall[:, 0:W], func=AF.Ln), writes=[dalln] + dbl)
                        op(ACT, lambda: nc.scalar.activation(out=dall[:, 0:W], in_=dall[:, 0:W], func=AF.Exp, scale=-1.0), writes=[dalln] + dbl)
                        op(DVE, lambda: nc.vector.tensor_tensor(out=ob[:, 0:W], in0=ob[:, 0:W], in1=dall[:, 0:W], op=ALU.mult), reads=[dalln], writes=[obn] + obl)

                    st(0, 0)
                    st(1, 0)
                    for j in range(8):
                        if j + 1 < 8:
                            st(0, j + 1)
                            st(1, j + 1)
                        pv(0, j)
                        pv(1, j)
                    sample(0)
                    sample(1)
                    finalize(0)
                    finalize(1)

                def dn():
                    for ti, (c0, n) in enumerate(B_TILES):
                        for d in range(NCH):
                            pd = banks[4 + d % 3]; pdn = "bank%d" % (4 + d % 3)

                            def mmd(d=d, pd=pd, c0=c0, n=n):
                                for cc in range(2):
                                    i = nc.tensor.matmul(pd[:, 0:n], lhsT=wB[s][:, cc, d * 128:(d + 1) * 128], rhs=obs[cc][:, c0 - HALO:c0 - HALO + n],
                                                         start=(cc == 0), stop=(cc == 1))
                                return i
                            op(PE, mmd, reads=[slotres(s)] + obns, writes=[pdn])
                            op(DVE, lambda d=d, pd=pd, c0=c0, n=n: nc.vector.tensor_tensor(out=x_sb[:, d, c0:c0 + n], in0=pd[:, 0:n], in1=x_sb[:, d, c0:c0 + n], op=ALU.add),
                               reads=[pdn], writes=["x_%d_%d" % (ti, d)])
                            P.run_trickle(1)
                        P.run_after_gu()
                        if hook is not None:
                            hook(ti)
                if gi == 0:
                    qproj2(s)
                if P.kc_deferred is not None:
                    f = P.kc_deferred
                    P.kc_deferred = None
                    f()
                attention2()
                P.flush()
                if gi < 3:
                    qproj2(1 - s)
                dn()
            out.append((load, compute))
        return out

    def prologue(ps):
        P.drain()
        P.fence()
        nblk = (T + 127) // 128
        ptiles = A_TILES if ps == 0 else B_TILES
        normed = set()
        for b in range(nblk):
            r0 = b * 128
            nr = min(128, T - r0)
            if ps == 1 and r0 + nr <= HALO:
                continue
            xb_ = xs[b % 2]; xbn = "xs%d" % (b % 2)
            dma(SP, xb_[0:nr, 0:D], xin[ps, r0:r0 + nr, :], writes=[xbn], track=xbn)
            for half in range(2):
                bk = banks[2 + half]; bkn = "bank%d" % (2 + half)

                def trx(half=half, bk=bk, nr=nr, xb_=xb_):
                    for kk in range(4):
                        k = half * 4 + kk
                        i = nc.tensor.transpose(out=bk[:, kk * 128:kk * 128 + nr], in_=xb_[0:nr, k * 128:(k + 1) * 128], identity=ident[0:nr, 0:nr])
                    return i
                op(PE, trx, reads=[xbn, "ident"], writes=[bkn])
                xres = [nm for ti, (c0, n) in enumerate(A_TILES if ps == 0 else B_TILES) if c0 < r0 + nr and r0 < c0 + n
                        for nm in XN(ti, range(half * 4, half * 4 + 4))]
                src = bk[:].rearrange("p (k c) -> p k c", k=4)[:, :, 0:nr]
                if half == 0:
                    op(DVE, lambda src=src, r0=r0, nr=nr: nc.vector.tensor_copy(out=x_sb[:, 0:4, r0:r0 + nr], in_=src), reads=[bkn], writes=xres)
                else:
                    op(ACT, lambda src=src, r0=r0, nr=nr: nc.scalar.copy(out=x_sb[:, 4:8, r0:r0 + nr], in_=src), reads=[bkn], writes=xres)
            for ti, (c0, n) in enumerate(ptiles):
                if ti not in normed and c0 + n <= r0 + nr:
                    normed.add(ti)
                    norm_tile(ti, c0, n, G1)

        dma(SP, k_s_o[ps, :, 0:127, :], ckd[ps, :, 1:128, :], writes=["kso"], track="kso")
        dma(SP, v_s_o[ps, :, 0:127, :], cvd[ps, :, 1:128, :], writes=["vso"], track="vso")
        for l in range(2):
            dma(SP, conv_s_o[ps, l, :, 0, :], stcd[ps, l].rearrange("(s r) d -> s r d", r=2)[:, 1, :], writes=["cso"], track="cso")
        dma(SP, xs[1][:, 0:512], mask40d[ps], writes=["xs1"], track="xs1")
        op(DVE, lambda: nc.vector.tensor_copy(out=mask40[:], in_=xs[1][:, 0:512]), reads=["xs1"], writes=["mask40"])
        for l in range(2):
            dma(SP, xs[l][0:2 * NS, 0:D], stcd[ps, l], writes=["xs%d" % l], track="xs%d" % l)
            bk = banks[l]

            def trs(l=l, bk=bk):
                for j in range(NCH):
                    i = nc.tensor.transpose(out=bk[:, j * 16:(j + 1) * 16], in_=xs[l][0:2 * NS, j * 128:(j + 1) * 128], identity=ident[0:2 * NS, 0:2 * NS])
                return i
            op(PE, trs, reads=["xs%d" % l, "ident"], writes=["bank%d" % l])
            op(DVE, lambda l=l, bk=bk: nc.vector.tensor_copy(out=st_fm[:, l, :, :], in_=bk[:, 0:128].rearrange("p (j c) -> p j c", j=NCH)),
               reads=["bank%d" % l], writes=["st_fm"])
        dma(SP, pos_sb[:], posd[ps], writes=["pos"], track="pos")
        for (tab, tname, off) in ((sinT, "sinT", 0.0), (cosT, "cosT", 0.25)):
            for (c0, n) in A_TILES:
                op(DVE, lambda c0=c0, n=n, off=off: nc.vector.tensor_scalar(out=t1[:, 0:n], in0=pos_sb[:, c0:c0 + n], scalar1=sm(FQ), scalar2=off, op0=ALU.mult, op1=ALU.add),
                   reads=["pos", "smalls"], writes=["t1"])
                op(DVE, lambda n=n: nc.vector.tensor_scalar(out=t2[:, 0:n], in0=t1[:, 0:n], scalar1=MAGIC, scalar2=MAGIC, op0=ALU.add, op1=ALU.subtract),
                   reads=["t1"], writes=["t2"])
                op(DVE, lambda n=n: nc.vector.tensor_tensor(out=t1[:, 0:n], in0=t1[:, 0:n], in1=t2[:, 0:n], op=ALU.subtract), reads=["t1", "t2"], writes=["t1"])
                op(ACT, lambda c0=c0, n=n, tab=tab: nc.scalar.activation(out=tab[:, c0:c0 + n], in_=t1[:, 0:n], func=AF.Sin, scale=TWO_PI_SAFE),
                   reads=["t1"], writes=[tname])
    def epilogue(ps):
        P.drain()
        P.fence()
        blocks = [(HALO + 128 * b, 128, y_own[ps, 128 * b:128 * (b + 1), :]) for b in range(8)] + [(HALO + OWN, NS, y_s[ps])]
        for bi, (c0, nr, dst) in enumerate(blocks):
            stg = xs[bi % 2]; stn = "xs%d" % (bi % 2)
            for half in range(2):
                bk = banks[half]; bkn = "bank%d" % half

                def tro(half=half, bk=bk, c0=c0, nr=nr):
                    for kk in range(4):
                        k = half * 4 + kk
                        i = nc.tensor.transpose(out=bk[0:nr, kk * 128:(kk + 1) * 128], in_=x_sb[:, k, c0:c0 + nr], identity=ident[:])
                    return i
                op(PE, tro, reads=XN(0) + XN(1) + XN(2) + ["ident"], writes=[bkn])
                if half == 0:
                    op(DVE, lambda bk=bk, nr=nr, stg=stg: nc.vector.tensor_copy(out=stg[0:nr, 0:512], in_=bk[0:nr, :]), reads=[bkn], writes=[stn])
                else:
                    op(ACT, lambda bk=bk, nr=nr, stg=stg: nc.scalar.copy(out=stg[0:nr, 512:1024], in_=bk[0:nr, :]), reads=[bkn], writes=[stn])
            dma(SP, dst, stg[0:nr, 0:D], reads=[stn], track=stn)
        for l in range(2):
            stg = xs[l]; stn = "xs%d" % l
            for half in range(2):
                bk = banks[2 + half]; bkn = "bank%d" % (2 + half)

                def trv(half=half, bk=bk, l=l):
                    for kk in range(4):
                        j = half * 4 + kk
                        i = nc.tensor.transpose(out=bk[0:NS + 2, kk * 128:(kk + 1) * 128], in_=sv[:, l, j, :], identity=ident[:])
                    return i
                op(PE, trv, reads=["sv", "ident"], writes=[bkn])
                op(DVE, lambda bk=bk, half=half, stg=stg: nc.vector.tensor_copy(out=stg[0:NS + 2, half * 512:(half + 1) * 512], in_=bk[0:NS + 2, :]),
                   reads=[bkn], writes=[stn])
            P.dma_batch(SP, [(conv_s_o[ps, l, :, 1, :], stg[0:NS, 0:D]), (conv_p_o[ps, l], stg[NS:NS + 2, 0:D])], reads=[stn], writes=["cso"], track=stn)

    seq = []

    def add_stage(tag, groups, ps, tiles_id, gcol):
        for gi, (load, compute) in enumerate(groups):
            seq.append((tag, load, compute, ps, gi == 0, gi == len(groups) - 1, tiles_id, gcol))

    for ps in range(NPASS):
        for l in range(4):
            tid = "A" if (l < 2 and ps == 0) else "B"
            tiles = A_TILES if tid == "A" else B_TILES
            if l == 2:
                add_stage("kv", kv_group(ps, A_TILES if ps == 0 else B_TILES), ps, "A" if ps == 0 else "B", GKV)
            add_stage("f1", ffn_groups(l, 1, tiles), ps, tid, G1 + 8 * l)
            if l < 2:
                add_stage("mA", mixA_groups(l, ps, tiles), ps, tid, GM + 8 * l)
            else:
                add_stage("mB", mixB_groups(l - 2, l, ps), ps, tid, GM + 8 * l)
            add_stage("f2", ffn_groups(l, 2, tiles), ps, tid, G2 + 8 * l)

    per_pass = len(seq) // NPASS
    seq[0][1](0)
    hooked = False
    for i, (tag, load, compute, ps, first, last, tid, gcol) in enumerate(seq):
        s = i % 2
        if i % per_pass == 0:
            prologue(ps)
        if i + 1 < len(seq):
            nxt = seq[i + 1][1]
            P.on_flush.append(lambda nxt=nxt, s2=(i + 1) % 2: nxt(s2))
        if tag == "kv":
            P.drain()
            if ps == 0:
                P.fence()
        P.mark('%s_p%d_%d' % (tag, ps, i))
        P.skip_norm = (hooked and first) or (i % per_pass == 0)
        hooked = False
        P.hook = None
        if last and tag != "kv" and i + 1 < len(seq) and (i + 1) % per_pass != 0 and (stop_after is None or i + 1 < stop_after):
            ntag, _, _, _, nfirst, _, ntid, ngcol = seq[i + 1]
            if nfirst and ntid == tid:
                tl = A_TILES if tid == "A" else B_TILES
                P.hook = (lambda ti, tl=tl, ngcol=ngcol: norm_hook(ti, tl[ti][0], tl[ti][1], ngcol))
                hooked = True
        compute(s)
        P.hook = None
        if tag == "kv":
            P.drain()
            if ps == 0:
                P.fence()
        if (i + 1) % per_pass == 0:
            epilogue(ps)
        if stop_after is not None and i + 1 >= stop_after:
            P.flush()
            if (i + 1) % per_pass != 0:
                epilogue(ps)
            break
    P.drain()
    P.fence()
    for r in P.res.values():
        if r.dsem is not None and r.dcount > 0:
            SP.h.wait_ge(r.dsem, r.dcount)
    P.es.close()
    nc._marks = P.marks
    return nc


_HEAD_PAIR = []
for _m in range(2):
    for _i in range(4):
        _HEAD_PAIR.append((8 * _m + _i, 8 * _m + 4 + _i))


def _partner(d):
    return d + 8 if d < 8 else (d - 8 if d < 16 else d)


def _prep_shared(inp):
    f = lambda a: np.ascontiguousarray(a, dtype=np.float32)
    w_q, w_o, w_kv = inp["w_q"], inp["w_o"], inp["w_kv"]
    colperm = []
    colperm_s = []
    for (ha, hb) in _HEAD_PAIR:
        for hh in (ha, hb):
            colperm += [64 * hh + d for d in range(64)]
            colperm_s += [64 * hh + _partner(d) for d in range(64)]
    colperm = np.array(colperm); colperm_s = np.array(colperm_s)
    kperm_s = np.array([64 * g + _partner(d) for g in range(4) for d in range(64)])
    sh = {
        "w1g": f(inp["w_ffn1_gate"]), "w1u": f(inp["w_ffn1_up"]), "w1d": f(inp["w_ffn1_down"]),
        "w2g": f(inp["w_ffn2_gate"]), "w2u": f(inp["w_ffn2_up"]), "w2d": f(inp["w_ffn2_down"]),
        "win": f(inp["w_in_a"]), "wout": f(inp["w_out_a"]), "wkv": f(w_kv),
        "wks": f(w_kv[:, 0:256][:, kperm_s]),
        "wqp": f(w_q[:, :, colperm]), "wqs": f(w_q[:, :, colperm_s]), "wop": f(w_o[:, colperm, :]),
        "ident": np.eye(128, dtype=np.float32),
    }
    smalls = np.zeros((128, NSM), np.float32)
    for l in range(4):
        for k in range(8):
            smalls[:, 0 + l * 8 + k] = inp["g_ffn1"][l, k * 128:(k + 1) * 128]
            smalls[:, 32 + l * 8 + k] = inp["g_mix"][l, k * 128:(k + 1) * 128]
            smalls[:, 64 + l * 8 + k] = inp["g_ffn2"][l, k * 128:(k + 1) * 128]
    for k in range(8):
        smalls[:, 96 + k] = inp["g_kv"][k * 128:(k + 1) * 128]
    for l in range(2):
        for tap in range(3):
            for j in range(8):
                smalls[:, 104 + l * 24 + tap * 8 + j] = inp["conv_w"][l, tap, j * 128:(j + 1) * 128]
    d_idx = np.arange(128) % 64
    pd_idx = np.array([_partner(d) for d in d_idx])
    for j in range(2):
        smalls[:, 152 + j] = inp["g_qnorm"][j][d_idx]
        smalls[:, 154 + j] = inp["g_qnorm"][j][pd_idx]
    smalls[:, 156] = inp["g_knorm"][d_idx]
    smalls[:, 157] = inp["g_knorm"][pd_idx]
    inv_freq = (np.float32(500000.0) ** (-np.arange(0, 16, 2, dtype=np.float32) / np.float32(16))).astype(np.float64)
    fq = np.where(d_idx < 16, inv_freq[d_idx % 8] / (2 * math.pi), 0.0)
    smalls[:, 158] = fq.astype(np.float32)
    smalls[:, 159] = np.where(d_idx < 8, -1.0, np.where(d_idx < 16, 1.0, 0.0)).astype(np.float32)
    for j in range(2):
        for c, (ha, hb) in enumerate(_HEAD_PAIR):
            smalls[0:64, 160 + j * 8 + c] = inp["sinks"][j, ha]
            smalls[64:128, 160 + j * 8 + c] = inp["sinks"][j, hb]
    sh["smalls"] = smalls
    kk = np.arange(128)[:, None]
    qq = np.arange(128)[None, :]
    mprev = (kk > qq).astype(np.float32)
    mcur = (kk <= qq).astype(np.float32)
    sh["mask4"] = np.concatenate([mprev, mcur, mprev, mcur], axis=1)
    sh["_mask40_first"] = np.concatenate([np.zeros_like(mprev), mcur, np.zeros_like(mprev), mcur], axis=1)
    return sh


def _prep_core(inp, sh, core):
    xin = np.zeros((NPASS, T, D), np.float32)
    pos = np.zeros((NPASS, 128, T), np.float32)
    mask40 = np.zeros((NPASS, 128, 512), np.float32)
    stc = np.zeros((NPASS, 2, 2 * NS, D), np.float32)
    ck = np.zeros((NPASS, NS, 128, 256), np.float32)
    cv = np.zeros((NPASS, NS, 128, 256), np.float32)
    for ps in range(NPASS):
        v = core * NPASS + ps
        b, seg = v // 8, v % 8
        s0 = seg * OWN
        lo = s0 - HALO
        if lo >= 0:
            xin[ps, 0:HALO] = inp["x_prompt"][b, lo:s0]
            mask40[ps] = sh["mask4"]
        else:
            mask40[ps] = sh["_mask40_first"]
        xin[ps, HALO:HALO + OWN] = inp["x_prompt"][b, s0:s0 + OWN]
        xin[ps, HALO + OWN:] = inp["x_sample"][v * NS:(v + 1) * NS, 0]
        p = np.concatenate([np.maximum(np.arange(lo, s0 + OWN), 0), np.full(NS, 8192)]).astype(np.float32)
        pos[ps] = p[None, :]
        stc[ps] = inp["state_conv"][:, v * NS:(v + 1) * NS].reshape(2, 2 * NS, D)
        ck[ps] = inp["cache_k"][v * NS:(v + 1) * NS].reshape(NS, 128, 256)
        cv[ps] = inp["cache_v"][v * NS:(v + 1) * NS].reshape(NS, 128, 256)
    m = {k: v for k, v in sh.items() if not k.startswith("_")}
    m.update({"xin": xin, "pos": pos, "mask40": mask40, "stc": stc, "ck": ck, "cv": cv})
    return m


_NC_CACHE = {}


def kernel(**inputs):
    inp = {k: np.asarray(v) for k, v in inputs.items()}
    sh = _prep_shared(inp)
    in_maps = [_prep_core(inp, sh, c) for c in range(NCORES)]
    if "nc" not in _NC_CACHE:
        _NC_CACHE["nc"] = build_program()
    res = run_bass_kernel_spmd(_NC_CACHE["nc"], in_maps, core_ids=list(range(NCORES)))
    R = res.results
    B, S = 2, 8192
    y_prompt = np.zeros((B, S, D), np.float32)
    y_sample = np.zeros((128, 1, D), np.float32)
    conv_p = np.zeros((2, B, 2, D), np.float32)
    k_p = np.zeros((B, 128, 4, 64), np.float32)
    v_p = np.zeros((B, 128, 4, 64), np.float32)
    conv_s = np.zeros((2, 128, 2, D), np.float32)
    k_s = np.zeros((128, 128, 4, 64), np.float32)
    v_s = np.zeros((128, 128, 4, 64), np.float32)
    for core in range(NCORES):
        r = R[core]
        for ps in range(NPASS):
            v = core * NPASS + ps
            b, seg = v // 8, v % 8
            y_prompt[b, seg * OWN:(seg + 1) * OWN] = r["y_own"][ps]
            y_sample[v * NS:(v + 1) * NS, 0] = r["y_s"][ps]
            conv_s[:, v * NS:(v + 1) * NS] = r["conv_s_o"][ps]
            k_s[v * NS:(v + 1) * NS] = r["k_s_o"][ps].reshape(NS, 128, 4, 64)
            v_s[v * NS:(v + 1) * NS] = r["v_s_o"][ps].reshape(NS, 128, 4, 64)
            if seg == 7:
                conv_p[:, b] = r["conv_p_o"][ps]
                k_p[b] = r["k_p_o"][ps].reshape(128, 4, 64)
                v_p[b] = r["v_p_o"][ps].reshape(128, 4, 64)
    return (y_prompt, y_sample, conv_p, k_p, v_p, conv_s, k_s, v_s)
```
